# Optimizing a Trainium2 kernel written in Bass

```python
import math
import jax, jax.numpy as jnp
from jax import lax
import numpy as np

D_MODEL = 2048
BATCH = 4
SEQ = 2048
DEPTH = 4
DEC_BATCH = 4
DEC_SEQ = 4096
PAST_LEN = 128

POOL_WIDTH = D_MODEL // 2
POOL_WINDOWS = (2, 4, 8, 16)
POOL_GROUP = POOL_WIDTH // len(POOL_WINDOWS)
HEAD_DIM = 128
ATTN_GROUPS = ((128, 1), (512, 4), (2048, 16))
HEADS_PER_GROUP = 4
N_HEADS = HEADS_PER_GROUP * len(ATTN_GROUPS)
ATTN_WIDTH = N_HEADS * HEAD_DIM
BAND_BLOCK = 64
ROPE_THETA = 10000.0
NORM_EPS = 1e-6
NEG_BIG = -1e30
SPLIT_SIZES = (POOL_WIDTH, POOL_WIDTH, ATTN_WIDTH, ATTN_WIDTH, ATTN_WIDTH, ATTN_WIDTH, D_MODEL, D_MODEL)
IN_WIDTH = sum(SPLIT_SIZES)

kernel_name = "gated_pool_dilated_attn_encoder"


def rmsnorm(x, gain):
    xf = x.astype(jnp.float32)
    xf = xf * lax.rsqrt(jnp.mean(xf * xf, axis=-1, keepdims=True) + NORM_EPS)
    return (xf * gain.astype(jnp.float32)).astype(x.dtype)


def rope(x, pos):
    half = HEAD_DIM // 2
    inv = ROPE_THETA ** (-jnp.arange(half, dtype=jnp.float32) / half)
    ang = pos[:, None] * inv[None, :]
    cos = jnp.cos(ang)[None, :, None, :]
    sin = jnp.sin(ang)[None, :, None, :]
    xf = x.astype(jnp.float32)
    x1, x2 = xf[..., :half], xf[..., half:]
    out = jnp.concatenate([x1 * cos - x2 * sin, x2 * cos + x1 * sin], axis=-1)
    return out.astype(x.dtype)


def multiscale_pool(h, w_grp, scale):
    B, S, C = h.shape
    hf = h.astype(jnp.float32)
    cs = jnp.concatenate([jnp.zeros((B, 1, C), jnp.float32), jnp.cumsum(hf, axis=1)], axis=1)
    t = jnp.arange(S)
    outs = []
    for g, w in enumerate(POOL_WINDOWS):
        sl = slice(g * POOL_GROUP, (g + 1) * POOL_GROUP)
        cs_g = cs[..., sl]
        lo = jnp.clip(t - w // 2, 0, S)
        hi = jnp.clip(t + w // 2, 0, S)
        cnt = (hi - lo).astype(jnp.float32)[None, :, None]
        mean = (cs_g[:, hi] - cs_g[:, lo]) / cnt
        outs.append(mean - hf[..., sl])
    p = jnp.stack(outs, axis=2).astype(h.dtype)
    y = jnp.einsum('bsgc,gcd->bsgd', p, w_grp).reshape(B, S, C)
    return y * scale


def dilated_band_attention(q, k, v, dil, radius):
    B, S, H, Dh = q.shape
    L = S // dil
    Lp = -(-L // BAND_BLOCK) * BAND_BLOCK
    nb = Lp // BAND_BLOCK

    def to_cls(t):
        return t.reshape(B, L, dil, H, Dh).transpose(0, 2, 3, 1, 4)

    qc, kc, vc = to_cls(q), to_cls(k), to_cls(v)
    qb = jnp.pad(qc, ((0, 0), (0, 0), (0, 0), (0, Lp - L), (0, 0))).reshape(B, dil, H, nb, BAND_BLOCK, Dh)

    def band(t):
        tp = jnp.pad(t, ((0, 0), (0, 0), (0, 0), (BAND_BLOCK, Lp - L + BAND_BLOCK), (0, 0)))
        tb = tp.reshape(B, dil, H, nb + 2, BAND_BLOCK, Dh)
        return jnp.concatenate([tb[:, :, :, :-2], tb[:, :, :, 1:-1], tb[:, :, :, 2:]], axis=4)

    kb, vb = band(kc), band(vc)
    blk = jnp.arange(nb)[:, None]
    qi = blk * BAND_BLOCK + jnp.arange(BAND_BLOCK)[None, :]
    kj = (blk - 1) * BAND_BLOCK + jnp.arange(3 * BAND_BLOCK)[None, :]
    diff = kj[:, None, :] - qi[:, :, None]
    mask = (jnp.abs(diff) <= radius) & (kj[:, None, :] >= 0) & (kj[:, None, :] < L)

    s = jnp.einsum('bdhnqc,bdhnkc->bdhnqk', qb, kb, preferred_element_type=jnp.float32) / math.sqrt(Dh)
    s = jnp.where(mask, s, NEG_BIG)
    m = jnp.max(s, axis=-1, keepdims=True)
    p = jnp.exp(s - m)
    l = jnp.sum(p, axis=-1, keepdims=True)
    o = jnp.einsum('bdhnqk,bdhnkc->bdhnqc', p, vb.astype(jnp.float32)) / l
    lse = (m + jnp.log(l))[..., 0]
    o = o.reshape(B, dil, H, Lp, Dh)[:, :, :, :L].transpose(0, 3, 1, 2, 4).reshape(B, S, H, Dh)
    lse = lse.reshape(B, dil, H, Lp)[:, :, :, :L].transpose(0, 3, 1, 2).reshape(B, S, H)
    return o.astype(q.dtype), lse


def dilated_mixture_attention(q, k, v):
    B, S, _ = q.shape
    pos = jnp.arange(S, dtype=jnp.float32)
    q = rope(q.reshape(B, S, N_HEADS, HEAD_DIM), pos)
    k = rope(k.reshape(B, S, N_HEADS, HEAD_DIM), pos)
    v = v.reshape(B, S, N_HEADS, HEAD_DIM)
    outs, lses = [], []
    for g, (win, dil) in enumerate(ATTN_GROUPS):
        sl = slice(g * HEADS_PER_GROUP, (g + 1) * HEADS_PER_GROUP)
        o, lse = dilated_band_attention(q[:, :, sl], k[:, :, sl], v[:, :, sl], dil, win // (2 * dil))
        outs.append(o)
        lses.append(lse)
    wts = jax.nn.softmax(jnp.stack(lses, axis=0), axis=0)
    o = jnp.concatenate([outs[g] * wts[g][..., None].astype(q.dtype) for g in range(len(ATTN_GROUPS))], axis=2)
    return o.reshape(B, S, ATTN_WIDTH)


def encoder_layer(x, c, norm_gain, w_ada, b_ada, w_in, w_pool_grp, pool_scale, w_proj_pool, w_proj_attn, w_out):
    mod = c @ w_ada + b_ada
    shift, scale, gate = jnp.split(mod, 3, axis=-1)
    h = rmsnorm(x, norm_gain) * (1 + scale[:, None, :]) + shift[:, None, :]
    z = h @ w_in
    idx = list(np.cumsum(SPLIT_SIZES)[:-1])
    pool_in, pool_gate, q, k, v, attn_gate, g_pool, g_attn = jnp.split(z, idx, axis=-1)
    a = multiscale_pool(pool_in, w_pool_grp, pool_scale) * jax.nn.silu(pool_gate)
    b = dilated_mixture_attention(q, k, v) * jax.nn.silu(attn_gate)
    merged = jax.nn.sigmoid(g_pool) * (a @ w_proj_pool) + jax.nn.sigmoid(g_attn) * (b @ w_proj_attn)
    return x + gate[:, None, :] * (merged @ w_out)


def run_trunk(x, c, norm_gain, w_ada, b_ada, w_in, w_pool_grp, pool_scale, w_proj_pool, w_proj_attn, w_out, final_gain):
    for i in range(DEPTH):
        x = encoder_layer(x, c, norm_gain[i], w_ada[i], b_ada[i], w_in[i], w_pool_grp[i], pool_scale[i],
                          w_proj_pool[i], w_proj_attn[i], w_out[i])
    return rmsnorm(x, final_gain)


def setup_inputs(seed: int = 0) -> dict:
    key = jax.random.key(seed)
    ks = jax.random.split(key, 14)
    f32 = jnp.float32
    nrm = lambda k, shape: jax.random.normal(k, shape, f32)
    return {
        "x_prompt": nrm(ks[0], (BATCH, SEQ, D_MODEL)),
        "x_sample": nrm(ks[1], (DEC_BATCH, DEC_SEQ, D_MODEL)),
        "c_prompt": nrm(ks[2], (BATCH, D_MODEL)),
        "c_sample": nrm(ks[3], (DEC_BATCH, D_MODEL)),
        "norm_gain": 1.0 + 0.02 * nrm(ks[4], (DEPTH, D_MODEL)),
        "w_ada": nrm(ks[5], (DEPTH, D_MODEL, 3 * D_MODEL)) * (0.2 * D_MODEL ** -0.5),
        "b_ada": 0.02 * nrm(ks[6], (DEPTH, 3 * D_MODEL)),
        "w_in": nrm(ks[7], (DEPTH, D_MODEL, IN_WIDTH)) * D_MODEL ** -0.5,
        "w_pool_grp": nrm(ks[8], (DEPTH, len(POOL_WINDOWS), POOL_GROUP, POOL_GROUP)) * POOL_GROUP ** -0.5,
        "pool_scale": 1.0 + 0.1 * nrm(ks[9], (DEPTH, POOL_WIDTH)),
        "w_proj_pool": nrm(ks[10], (DEPTH, POOL_WIDTH, D_MODEL)) * POOL_WIDTH ** -0.5,
        "w_proj_attn": nrm(ks[11], (DEPTH, ATTN_WIDTH, D_MODEL)) * ATTN_WIDTH ** -0.5,
        "w_out": nrm(ks[12], (DEPTH, D_MODEL, D_MODEL)) * D_MODEL ** -0.5,
        "final_gain": 1.0 + 0.02 * nrm(ks[13], (D_MODEL,)),
    }


def reference(x_prompt, x_sample, c_prompt, c_sample, norm_gain, w_ada, b_ada, w_in, w_pool_grp, pool_scale,
              w_proj_pool, w_proj_attn, w_out, final_gain):
    y_prompt = run_trunk(x_prompt, c_prompt, norm_gain, w_ada, b_ada, w_in, w_pool_grp, pool_scale,
                         w_proj_pool, w_proj_attn, w_out, final_gain)
    y_sample = run_trunk(x_sample, c_sample, norm_gain, w_ada, b_ada, w_in, w_pool_grp, pool_scale,
                         w_proj_pool, w_proj_attn, w_out, final_gain)
    return (y_prompt, y_sample)
```

```python
import math
from contextlib import ExitStack

import numpy as np
import ml_dtypes

import concourse.bass as bass
import concourse.mybir as mybir
from concourse.bass_utils import run_bass_kernel_spmd

F32 = mybir.dt.float32
BF16 = mybir.dt.bfloat16
AF = mybir.ActivationFunctionType
ALU = mybir.AluOpType

D = 2048
KC = 16
INW = 12288
POOLW = 1024
ATTW = 1536
NH = 12
GROUPS = ((128, 1), (512, 4), (2048, 16))
EPS = 1e-6
MASK_NEG = -30000.0
PADV = 1024
NB_T = 4
TT = NB_T * 128

CG_PI = (0, 1)
CG_PG = (2, 3)
CG_Q = (4, 5, 6)
CG_K = (7, 8, 9)
CG_V = (10, 11, 12)
CG_AG = (13, 14, 15)
CG_GP = (16, 17, 18, 19)
CG_GA = (20, 21, 22, 23)


class Sem:
    def __init__(self, nc, es, name):
        self.h = es.enter_context(nc.semaphore(name))
        self.v = 0
        self.name = name


class EngQ:
    def __init__(self, name):
        self.name = name
        self.ops = []
        self.waited = {}
        self.csem = None

    def wait(self, ev):
        if ev is None:
            return
        sem, val = ev
        if val <= 0 or self.waited.get(sem.name, 0) >= val:
            return
        self.waited[sem.name] = val
        self.ops.append(("w", sem, val))

    def op(self, fn):
        if self.csem is not None and self.name != "pe":
            self.sig(fn)
        else:
            self.ops.append(("i", fn, None, 0))

    def sig(self, fn):
        s = self.csem
        s.v += 1
        self.ops.append(("i", fn, s, 1))
        return (s, s.v)

    def dma(self, fn, sem):
        sem.v += 16
        self.ops.append(("i", fn, sem, 16))
        return (sem, sem.v)

    def run(self, eng):
        for o in self.ops:
            if o[0] == "w":
                eng.wait_ge(o[1].h, o[2])
            else:
                ins = o[1](eng)
                if o[2] is not None:
                    ins.then_inc(o[2].h, o[3])


def qk_perm():
    perm = []
    for pr in range(NH // 2):
        a, b = 2 * pr, 2 * pr + 1
        perm += list(range(a * 128, a * 128 + 64)) + list(range(b * 128, b * 128 + 64))
        perm += list(range(a * 128 + 64, a * 128 + 128)) + list(range(b * 128 + 64, b * 128 + 128))
    return np.array(perm)


def rope_tables(S):
    half = 64
    inv = (10000.0 ** (-np.arange(half, dtype=np.float32) / np.float32(half))).astype(np.float32)
    pos = np.arange(S, dtype=np.float32)
    ang = (pos[None, :] * inv[:, None]).astype(np.float32)
    cos = np.cos(ang).astype(np.float32)
    sin = np.sin(ang).astype(np.float32)
    return np.concatenate([cos, cos], 0), np.concatenate([sin, sin], 0)


def pool_mats(S, s_real):
    wins = (2, 4, 8, 16)
    nb = S // 128
    blocks = (0, 1 if nb > 2 else 0, nb // 2 - 1, nb - 1)
    out = np.zeros((4, 4, 3, 128, 128), np.float32)
    for si, b in enumerate(blocks):
        for wi, w in enumerate(wins):
            h = w // 2
            for tl in range(128):
                t = b * 128 + tl
                if t >= s_real:
                    out[si, wi, 1, tl, tl] = 0.0
                    continue
                lo, hi = max(0, t - h), min(s_real, t + h)
                cnt = hi - lo
                for tp in range(lo, hi):
                    rel = tp // 128 - b + 1
                    out[si, wi, rel, tp % 128, tl] += 1.0 / cnt
                out[si, wi, 1, tl, tl] -= 1.0
    return out.astype(ml_dtypes.bfloat16)


def attn_batches(S):
    plan = []
    for (win, dil) in GROUPS:
        L = S // dil
        nbc = L // 128
        items = [(c, qb) for c in range(dil) for qb in range(nbc)]
        plan.append([items[i:i + 4] for i in range(0, len(items), 4)])
    return plan


def mask_variants(S):
    sigs = []
    idx = []
    for gi, (win, dil) in enumerate(GROUPS):
        L = S // dil
        nbc = L // 128
        gidx = []
        for batch in attn_batches(S)[gi]:
            sa = tuple("first" if qb == 0 else "mid" for (c, qb) in batch)
            sb = tuple("last" if qb == nbc - 1 else ("half" if (2 * (qb + 1) == nbc) else "mid") for (c, qb) in batch)
            pair = []
            for kind, sg in (("A", sa), ("B", sb)):
                key = (kind,) + sg
                if key not in sigs:
                    sigs.append(key)
                pair.append(sigs.index(key))
            gidx.append(tuple(pair))
        idx.append(gidx)
    return idx, sigs


def mask_table(S, s_real):
    _, sigs = mask_variants(S)
    tab = np.zeros((len(sigs), 128, 512), np.float32)
    p = np.arange(128)[:, None]
    c = np.arange(128)[None, :]
    for i, key in enumerate(sigs):
        kind = key[0]
        for j, v in enumerate(key[1:]):
            if kind == "A":
                m = (p >= c).astype(np.float32)
                if v == "first":
                    m = m * (p >= 64)
            else:
                m = (p <= c).astype(np.float32)
                if v == "last":
                    m = m * (p < 64)
                if v == "half" and s_real < S:
                    m = m * (p < 64)
            tab[i, :, j * 128:(j + 1) * 128] = (1.0 - m) * MASK_NEG
    return tab.astype(ml_dtypes.bfloat16)


class Prog:
    def __init__(self, S=4096, depth=4, dbg=False):
        self.S, self.depth, self.dbg = S, depth, dbg
        self.NB = S // 128
        self.NT = S // TT
        self.nc = nc = bass.Bass("TRN2", target_bir_lowering=False)
        scr = "ExternalOutput" if dbg else "Internal"

        def dram(name, shape, dt, kind):
            return nc.dram_tensor(name, list(shape), dt, kind=kind).ap()

        self.x_in = dram("x", [S, D], F32, "ExternalInput")
        self.c_in = dram("c", [D], F32, "ExternalInput")
        self.ng_in = dram("norm_gain", [depth, D], F32, "ExternalInput")
        self.wada_in = dram("w_ada", [depth, D, 3 * D], F32, "ExternalInput")
        self.bada_in = dram("b_ada", [depth, 3 * D], F32, "ExternalInput")
        self.win_in = dram("w_in", [depth, D, INW], F32, "ExternalInput")
        self.wgrp_in = dram("w_pool_grp", [depth, 4, 256, 256], F32, "ExternalInput")
        self.psc_in = dram("pool_scale", [depth, POOLW], F32, "ExternalInput")
        self.wpp_in = dram("w_proj_pool", [depth, POOLW, D], F32, "ExternalInput")
        self.wpa_in = dram("w_proj_attn", [depth, ATTW, D], F32, "ExternalInput")
        self.wout_in = dram("w_out", [depth, D, D], F32, "ExternalInput")
        self.fg_in = dram("final_gain", [D], F32, "ExternalInput")
        self.cos_in = dram("rope_cos", [128, S], F32, "ExternalInput")
        self.sin_in = dram("rope_sin", [128, S], F32, "ExternalInput")
        self.pmat_in = dram("pool_mats", [4, 4, 3, 128, 128], BF16, "ExternalInput")
        self.midx, self.msigs = mask_variants(S)
        self.NMV = len(self.msigs)
        self.mask_in = dram("attn_masks", [self.NMV, 128, 512], BF16, "ExternalInput")
        self.ident_in = dram("ident", [128, 128], F32, "ExternalInput")
        self.tval_in = dram("tile_valid", [128, S // TT], F32, "ExternalInput")
        self.y_out = dram("y", [S, D], F32, "ExternalOutput")

        self.wi_bf = dram("wi_bf", [depth, 24, 128, KC, 512], BF16, "Internal")
        self.wg_bf = dram("wg_bf", [depth, 128, 4, 2, 256], BF16, "Internal")
        self.wpp_bf = dram("wpp_bf", [depth, 4, 128, 8, 512], BF16, "Internal")
        self.wpa_bf = dram("wpa_bf", [depth, 4, 128, 12, 512], BF16, "Internal")
        self.wo_bf = dram("wo_bf", [depth, 4, 128, KC, 512], BF16, "Internal")
        self.mod_d = dram("mod_d", [depth, 3, D], F32, scr)
        self.hT_d = dram("hT_d", [KC, 128, S], BF16, scr)
        self.qT_d = dram("qT_d", [NH, 128, S], BF16, scr)
        self.kT_d = dram("kT_d", [NH, 128, S], BF16, scr)
        self.v_d = dram("v_d", [S + 2 * PADV, ATTW], BF16, scr)
        self.pi_d = dram("pi_d", [S, POOLW], BF16, scr)
        self.bT_d = dram("bT_d", [NH, 128, S], BF16, scr)
        self.x_d = dram("x_d", [S, D], F32, scr)

        self.es = ExitStack()
        self.sems = {}
        self.cast_ev = [None] * depth
        self.cast_pending = {}
        self.cast_evs = {}
        self.cast_slot_last = {}
        self.cast_k = 0

    def sem(self, name):
        if name not in self.sems:
            self.sems[name] = Sem(self.nc, self.es, name)
        return self.sems[name]

    def new_queues(self):
        self.PE, self.ACT, self.DVE, self.POOL, self.SP = EngQ("pe"), EngQ("act"), EngQ("dve"), EngQ("pool"), EngQ("sp")
        for q in (self.PE, self.ACT, self.DVE, self.POOL):
            q.csem = self.sem("c_" + q.name)
        return self.PE, self.ACT, self.DVE, self.POOL, self.SP

    def emit_block(self):
        with self.nc.Block() as block:
            @block.tensor
            def _(e):
                self.PE.run(e)

            @block.scalar
            def _(e):
                self.ACT.run(e)

            @block.vector
            def _(e):
                self.DVE.run(e)

            @block.gpsimd
            def _(e):
                self.POOL.run(e)

            @block.sync
            def _(e):
                self.SP.run(e)

    def cast_list(self, l):
        lst = []
        order = list(CG_Q + CG_K + CG_V + CG_PI) + [cg for cg in range(24) if cg not in (CG_Q + CG_K + CG_V + CG_PI)]
        for cg in order:
            lst.append(lambda e, cg=cg: e.dma_start(
                out=self.wi_bf[l, cg],
                in_=self.win_in[l][:, cg * 512:(cg + 1) * 512].rearrange("(kc p) c -> p kc c", p=128)))
        for g in range(4):
            lst.append(lambda e, g=g: e.dma_start(
                out=self.wg_bf[l][:, g], in_=self.wgrp_in[l, g].rearrange("(kc p) c -> p kc c", p=128)))
        for og in range(4):
            for dst, src in ((self.wpp_bf, self.wpp_in), (self.wpa_bf, self.wpa_in), (self.wo_bf, self.wout_in)):
                lst.append(lambda e, og=og, dst=dst, src=src: e.dma_start(
                    out=dst[l, og], in_=src[l][:, og * 512:(og + 1) * 512].rearrange("(kc p) c -> p kc c", p=128)))
        return lst

    def feed_cast(self, l, n):
        if l >= self.depth:
            return
        if l not in self.cast_pending:
            self.cast_pending[l] = self.cast_list(l)
        evs = self.cast_evs.setdefault(l, {})
        for _ in range(n):
            if not self.cast_pending[l]:
                break
            fn = self.cast_pending[l].pop(0)
            slot = self.cast_k % 3
            self.cast_k += 1
            self.POOL.wait(self.cast_slot_last.get(slot))
            ev = self.POOL.dma(fn, self.sem(f"s_castslot{slot}"))
            self.cast_slot_last[slot] = ev
            evs[slot] = ev

    def wait_casts(self, q, l):
        for ev in self.cast_evs.get(l, {}).values():
            q.wait(ev)

    def feed2(self, l, n):
        for _ in range(n):
            if self.cast_pending.get(l) is None and l < self.depth:
                self.cast_pending[l] = self.cast_list(l)
            if l < self.depth and self.cast_pending[l]:
                self.feed_cast(l, 1)
            else:
                self.feed_cast(l + 1, 1)

    def dma_ring(self, prefix, n):
        return [self.sem(f"{prefix}{i}") for i in range(n)]

    def phase0(self):
        nc, depth, S = self.nc, self.depth, self.S
        PE, ACT, DVE, POOL, SP = self.new_queues()
        sem = self.sem
        with ExitStack() as es0:
            def sb0(name, shape, dt):
                return es0.enter_context(nc.sbuf_tensor(name, list(shape), dt))
            zeros_bf = sb0("zeros_bf", [128, ATTW], BF16)
            cT = sb0("cT", [128, KC], F32)
            slabs = [sb0(f"ada_slab{i}", [128, 3072], F32) for i in range(3)]
            modrow = sb0("modrow", [1, 3 * D], F32)
            badarow = sb0("badarow", [1, 3 * D], F32)
            gainrow = sb0("gainrow", [1, D], F32)
            acc = [es0.enter_context(nc.psum_tensor(f"ada_acc{j}", [128, 512], F32)) for j in range(6)]

            s_vpad = sem("s_vpad")
            ez = POOL.sig(lambda e: e.memset(zeros_bf[:], 0.0))
            POOL.wait(ez)
            for r0 in list(range(0, PADV, 128)) + list(range(PADV + S, PADV + S + PADV, 128)):
                evp = POOL.dma(lambda e, r0=r0: e.dma_start(out=self.v_d[r0:r0 + 128, :], in_=zeros_bf[:, :]), s_vpad)
            self.feed_cast(0, 11)
            POOL.wait(evp)

            s_c, s_row, s_st = sem("s0_c"), sem("s0_row"), sem("s0_st")
            s_ld = self.dma_ring("ld_a", 3)
            evc = SP.dma(lambda e: e.dma_start(out=cT[:], in_=self.c_in.rearrange("(kc p) -> p kc", p=128),
                                               allow_slow_non_contiguous=True), s_c)
            slab_rel = {}
            nslab = 0
            ev_st = None
            ev_acc_free = None
            for l in range(depth):
                if ev_st is not None:
                    SP.wait(ev_st)
                SP.dma(lambda e, l=l: e.dma_start(out=badarow[0:1, :], in_=self.bada_in[l:l + 1, :]), s_row)
                ev_row = SP.dma(lambda e, l=l: e.dma_start(out=gainrow[0:1, :], in_=self.ng_in[l:l + 1, :]), s_row)
                ev_last = None
                for hf in range(2):
                    ev_acc = []
                    for kc in range(KC):
                        k = nslab
                        nslab += 1
                        buf = slabs[k % 3]
                        if k >= 3:
                            SP.wait(slab_rel[k - 3])
                        ev_ld = SP.dma(lambda e, l=l, hf=hf, kc=kc, buf=buf: e.dma_start(
                            out=buf[:], in_=self.wada_in[l][kc * 128:(kc + 1) * 128, hf * 3072:(hf + 1) * 3072]), s_ld[k % 3])
                        PE.wait(ev_ld)
                        PE.wait(evc)
                        if kc == 0 and ev_acc_free is not None:
                            PE.wait(ev_acc_free)
                        for j in range(6):
                            fn = lambda e, j=j, kc=kc, buf=buf: e.matmul(
                                acc[j][0:1, :], lhsT=cT[:, kc:kc + 1], rhs=buf[:, j * 512:(j + 1) * 512],
                                start=(kc == 0), stop=(kc == KC - 1))
                            if kc == KC - 1:
                                ev_acc.append(PE.sig(fn))
                                if j == 5:
                                    slab_rel[k] = ev_acc[-1]
                            elif j == 5:
                                slab_rel[k] = PE.sig(fn)
                            else:
                                PE.op(fn)
                    DVE.wait(ev_row)
                    if ev_st is not None:
                        DVE.wait(ev_st)
                    for j in range(6):
                        DVE.wait(ev_acc[j])
                        c0 = hf * 3072 + j * 512
                        ev_last = DVE.sig(lambda e, j=j, c0=c0: e.tensor_tensor(
                            out=modrow[0:1, c0:c0 + 512], in0=acc[j][0:1, :], in1=badarow[0:1, c0:c0 + 512], op=ALU.add))
                    ev_acc_free = ev_last
                DVE.wait(ev_last)
                evg = DVE.sig(lambda e: e.scalar_tensor_tensor(
                    out=modrow[0:1, D:2 * D], in0=modrow[0:1, D:2 * D], scalar=1.0, in1=gainrow[0:1, :],
                    op0=ALU.add, op1=ALU.mult))
                SP.wait(evg)
                SP.dma(lambda e, l=l: e.dma_start(out=self.mod_d[l, 0:1, :], in_=modrow[0:1, D:2 * D]), s_st)
                SP.dma(lambda e, l=l: e.dma_start(out=self.mod_d[l, 1:2, :], in_=modrow[0:1, 0:D]), s_st)
                ev_st = SP.dma(lambda e, l=l: e.dma_start(out=self.mod_d[l, 2:3, :], in_=modrow[0:1, 2 * D:3 * D]), s_st)
            SP.wait(ev_st)
            self.emit_block()

    def pass1(self, l):
        nc, S, NT = self.nc, self.S, self.NT
        PE, ACT, DVE, POOL, SP = self.new_queues()
        sem = self.sem
        x_src = self.x_in if l == 0 else self.x_d
        with ExitStack() as es1:
            def sb(name, shape, dt):
                return es1.enter_context(nc.sbuf_tensor(f"{name}_L{l}", list(shape), dt))

            def psb(name):
                return es1.enter_context(nc.psum_tensor(f"{name}_L{l}", [128, 512], F32))

            ident = sb("p1_ident", [128, 128], F32)
            Gf = sb("p1_Gf", [128, KC], F32)
            SHf = sb("p1_SHf", [128, KC], F32)
            eps_t = sb("p1_eps", [128, 1], F32)
            tval = sb("p1_tval", [128, NT], F32)
            SHt = [sb(f"p1_SHt{i}", [128, KC], F32) for i in range(2)]
            xb = [sb(f"p1_xb{i}", [128, D], F32) for i in range(4)]
            junk = sb("p1_junk", [128, D], BF16)
            stat = sb("p1_stat", [128, 32], F32)
            xn = [sb(f"p1_xn{i}", [128, D], F32) for i in range(NB_T)]
            hT = [sb(f"p1_hT{i}", [128, KC, TT], BF16) for i in range(2)]
            slabs = [sb(f"p1_slab{i}", [128, KC, 512], BF16) for i in range(3)]
            cs = [sb(f"p1_cs{i}", [128, 2, TT], F32) for i in range(2)]
            rt = [sb(f"p1_rt{i}", [128, 4, TT], F32) for i in range(2)]
            qko = [sb(f"p1_qko{i}", [128, 2, TT], BF16) for i in range(2)]
            vto = [sb(f"p1_vto{i}", [128, 512], BF16) for i in range(3)]
            tpb = [psb(f"p1_tp{i}") for i in range(2)]
            qkb = [psb(f"p1_qk{i}") for i in range(4)]
            vb = [psb(f"p1_v{i}") for i in range(2)]

            s_cst = sem("s_cst")
            s_ldx = self.dma_ring("ld_x", 4)
            s_ldw = self.dma_ring("ld_w", 3)
            s_ldc = self.dma_ring("ld_c", 2)
            s_sth = self.dma_ring("st_h", 2)
            s_stq = self.dma_ring("st_q", 2)
            s_stv = self.dma_ring("st_v", 3)

            SP.dma(lambda e: e.dma_start(out=ident[:], in_=self.ident_in[:, :]), s_cst)
            SP.dma(lambda e: e.dma_start(out=tval[:], in_=self.tval_in[:, :]), s_cst)
            SP.dma(lambda e: e.dma_start(out=Gf[:], in_=self.mod_d[l, 0].rearrange("(kc p) -> p kc", p=128),
                                         allow_slow_non_contiguous=True), s_cst)
            ev_cst = SP.dma(lambda e: e.dma_start(out=SHf[:], in_=self.mod_d[l, 1].rearrange("(kc p) -> p kc", p=128),
                                                  allow_slow_non_contiguous=True), s_cst)
            if l not in self.cast_pending:
                self.cast_pending[l] = self.cast_list(l)
            n_issued = 40 - len(self.cast_pending[l])
            self.feed_cast(l, max(0, 11 - n_issued))
            self.wait_casts(SP, l)
            ev_eps = POOL.sig(lambda e: e.memset(eps_t[:], EPS))
            ACT.wait(ev_eps)

            cnt = dict(x=0, tp=0, slab=0, qk=0, rt=0, qko=0, v=0, vto=0, cs=0)
            rel = dict(x={}, tp={}, slab={}, qk={}, rt={}, qko={}, v={}, vto={}, cs={})
            hT_ready, hT_pe_done, hT_st = {}, {}, {}
            xn_ready, xn_free = {}, {}
            cs_buf, cs_ev = {}, {}

            def emit_norm(ti):
                for blk in range(NB_T):
                    gb = ti * NB_T + blk
                    t0 = gb * 128
                    k = cnt["x"]; cnt["x"] += 1
                    buf = xb[k % 4]
                    if k >= 4:
                        SP.wait(rel["x"][k - 4])
                    evx = SP.dma(lambda e, buf=buf, t0=t0: e.dma_start(out=buf[:], in_=x_src[t0:t0 + 128, :]), s_ldx[k % 4])
                    col = (gb % 8) * 3
                    ACT.wait(evx)
                    eva = ACT.sig(lambda e, buf=buf, col=col: e.activation(
                        out=junk[:], in_=buf[:], func=AF.Square, accum_out=stat[:, col:col + 1]))
                    ACT.wait(eva)
                    evs = ACT.sig(lambda e, col=col: e.activation(
                        out=stat[:, col + 1:col + 2], in_=stat[:, col:col + 1], func=AF.Sqrt, bias=eps_t[:, 0:1], scale=1.0 / D))
                    DVE.wait(evs)
                    evd = DVE.sig(lambda e, col=col: e.reciprocal(out=stat[:, col + 2:col + 3], in_=stat[:, col + 1:col + 2]))
                    DVE.wait(evd)
                    if ti > 0:
                        DVE.wait(xn_free[ti - 1])
                    evxn = DVE.sig(lambda e, buf=buf, blk=blk, col=col: e.tensor_scalar(
                        out=xn[blk][:], in0=buf[:], scalar1=stat[:, col + 2:col + 3], scalar2=None, op0=ALU.mult))
                    rel["x"][k] = evxn
                    xn_ready[ti] = evxn

            def emit_transposes(ti):
                hbuf = hT[ti % 2]
                evh = None
                for kc in range(KC):
                    k = cnt["tp"]; cnt["tp"] += 1
                    bank = tpb[k % 2]
                    if k >= 2:
                        PE.wait(rel["tp"][k - 2])
                    if kc == 0:
                        PE.wait(xn_ready[ti])
                        PE.wait(ev_cst)
                    for blk in range(NB_T):
                        fn = lambda e, bank=bank, blk=blk, kc=kc: e.transpose(
                            bank[:, blk * 128:(blk + 1) * 128], xn[blk][:, kc * 128:(kc + 1) * 128], ident[:])
                        if blk == NB_T - 1:
                            evt = PE.sig(fn)
                        else:
                            PE.op(fn)
                    DVE.wait(evt)
                    if kc == 0:
                        DVE.wait(ev_cst)
                        if ti >= 2:
                            DVE.wait(hT_pe_done[ti - 2])
                            DVE.wait(hT_st[ti - 2])
                        sht = SHt[ti % 2]
                        ev_sh = DVE.sig(lambda e, sht=sht, ti=ti: e.tensor_scalar(
                            out=sht[:, :], in0=SHf[:, :], scalar1=tval[:, ti:ti + 1], scalar2=None, op0=ALU.mult))
                        DVE.wait(ev_sh)
                    evh = DVE.sig(lambda e, bank=bank, kc=kc, hbuf=hbuf, sht=sht: e.tensor_scalar(
                        out=hbuf[:, kc, :], in0=bank[:, :], scalar1=Gf[:, kc:kc + 1], scalar2=sht[:, kc:kc + 1],
                        op0=ALU.mult, op1=ALU.add))
                    rel["tp"][k] = evh
                xn_free[ti] = evt
                hT_ready[ti] = evh
                ACT.wait(evh)
                hT_st[ti] = ACT.dma(lambda e, hbuf=hbuf, ti=ti: e.dma_start(
                    out=self.hT_d[:, :, ti * TT:(ti + 1) * TT].rearrange("kc p t -> p kc t"), in_=hbuf[:, :, :]), s_sth[ti % 2])

            def load_slab(cg):
                k = cnt["slab"]; cnt["slab"] += 1
                buf = slabs[k % 3]
                if k >= 3:
                    SP.wait(rel["slab"][k - 3])
                ev = SP.dma(lambda e, buf=buf, cg=cg: e.dma_start(out=buf[:], in_=self.wi_bf[l, cg]), s_ldw[k % 3])
                return k, buf, ev

            def load_cs(ti):
                k = cnt["cs"]; cnt["cs"] += 1
                buf = cs[k % 2]
                if k >= 2:
                    SP.wait(rel["cs"][k - 2])
                SP.dma(lambda e, buf=buf, ti=ti: e.dma_start(out=buf[:, 0, :], in_=self.cos_in[:, ti * TT:(ti + 1) * TT]), s_ldc[k % 2])
                ev = SP.dma(lambda e, buf=buf, ti=ti: e.dma_start(out=buf[:, 1, :], in_=self.sin_in[:, ti * TT:(ti + 1) * TT]), s_ldc[k % 2])
                cs_buf[ti], cs_ev[ti] = buf, ev
                return k

            def emit_qk(ti, cgs, cs_k=None):
                hbuf = hT[ti % 2]
                for cg in cgs:
                    ks, slab, evsl = load_slab(cg)
                    is_q = cg in CG_Q
                    dst = self.qT_d if is_q else self.kT_d
                    cgi = (cg - CG_Q[0]) if is_q else (cg - CG_K[0])
                    for pr in range(2):
                        k = cnt["qk"]; cnt["qk"] += 1
                        banks = (qkb[2 * (k % 2)], qkb[2 * (k % 2) + 1])
                        if k >= 2:
                            PE.wait(rel["qk"][k - 2])
                        PE.wait(evsl)
                        PE.wait(hT_ready[ti])
                        for ch in range(2):
                            c0 = (pr * 2 + ch) * 128
                            for kc in range(KC):
                                fn = lambda e, b=banks[ch], kc=kc, c0=c0, slab=slab, hbuf=hbuf: e.matmul(
                                    b[:, :], lhsT=slab[:, kc, c0:c0 + 128], rhs=hbuf[:, kc, :],
                                    start=(kc == 0), stop=(kc == KC - 1))
                                if kc == KC - 1 and ch == 1:
                                    evq = PE.sig(fn)
                                else:
                                    PE.op(fn)
                        if pr == 1:
                            rel["slab"][ks] = evq
                        hT_pe_done[ti] = evq
                        csb = cs_buf[ti]
                        kr = cnt["rt"]; cnt["rt"] += 1
                        tbuf = rt[kr % 2]
                        if kr >= 2:
                            DVE.wait(rel["rt"][kr - 2])
                        DVE.wait(evq)
                        DVE.wait(cs_ev[ti])
                        for (slot, bk, tb) in ((0, 0, 0), (1, 1, 1), (2, 1, 0), (3, 0, 1)):
                            fn = lambda e, slot=slot, bk=bk, tb=tb, tbuf=tbuf, banks=banks, csb=csb: e.tensor_tensor(
                                out=tbuf[:, slot, :], in0=banks[bk][:, :], in1=csb[:, tb, :], op=ALU.mult)
                            if slot == 3:
                                evr = DVE.sig(fn)
                            else:
                                DVE.op(fn)
                        rel["qk"][k] = evr
                        if cs_k is not None and cg == cgs[-1] and pr == 1:
                            rel["cs"][cs_k] = evr
                        ko = cnt["qko"]; cnt["qko"] += 1
                        obuf = qko[ko % 2]
                        if ko >= 2:
                            POOL.wait(rel["qko"][ko - 2])
                        POOL.wait(evr)
                        POOL.op(lambda e, tbuf=tbuf, obuf=obuf: e.tensor_tensor(
                            out=obuf[:, 0, :], in0=tbuf[:, 0, :], in1=tbuf[:, 1, :], op=ALU.subtract))
                        evo = POOL.sig(lambda e, tbuf=tbuf, obuf=obuf: e.tensor_tensor(
                            out=obuf[:, 1, :], in0=tbuf[:, 2, :], in1=tbuf[:, 3, :], op=ALU.add))
                        rel["rt"][kr] = evo
                        POOL.wait(evo)
                        hA = 2 * (cgi * 2 + pr)
                        for hd in range(2):
                            evst = POOL.dma(lambda e, obuf=obuf, hd=hd, hA=hA, dst=dst, ti=ti: e.dma_start(
                                out=dst[hA + hd, :, ti * TT:(ti + 1) * TT].rearrange("(hf d) t -> d hf t", hf=2),
                                in_=obuf[hd * 64:(hd + 1) * 64, :, :]), s_stq[ko % 2])
                        rel["qko"][ko] = evst

            def emit_v(ti, cgs):
                hbuf = hT[ti % 2]
                t0 = ti * TT
                for ci, cg in enumerate(cgs):
                    ks, slab, evsl = load_slab(cg)
                    for blk in range(NB_T):
                        k = cnt["v"]; cnt["v"] += 1
                        bank = vb[k % 2]
                        if k >= 2:
                            PE.wait(rel["v"][k - 2])
                        PE.wait(evsl)
                        PE.wait(hT_ready[ti])
                        for kc in range(KC):
                            fn = lambda e, bank=bank, kc=kc, blk=blk, slab=slab, hbuf=hbuf: e.matmul(
                                bank[:, :], lhsT=hbuf[:, kc, blk * 128:(blk + 1) * 128], rhs=slab[:, kc, :],
                                start=(kc == 0), stop=(kc == KC - 1))
                            if kc == KC - 1:
                                evv = PE.sig(fn)
                            else:
                                PE.op(fn)
                        if blk == NB_T - 1:
                            rel["slab"][ks] = evv
                        hT_pe_done[ti] = evv
                        ko = cnt["vto"]; cnt["vto"] += 1
                        obuf = vto[ko % 3]
                        if ko >= 3:
                            ACT.wait(rel["vto"][ko - 3])
                        ACT.wait(evv)
                        evo = ACT.sig(lambda e, bank=bank, obuf=obuf: e.activation(out=obuf[:], in_=bank[:, :], func=AF.Copy))
                        rel["v"][k] = evo
                        ACT.wait(evo)
                        r0 = t0 + blk * 128
                        if cg in CG_V:
                            g = cg - CG_V[0]
                            rel["vto"][ko] = ACT.dma(lambda e, obuf=obuf, r0=r0, g=g: e.dma_start(
                                out=self.v_d[PADV + r0:PADV + r0 + 128, g * 512:(g + 1) * 512], in_=obuf[:]), s_stv[ko % 3])
                        else:
                            g = cg - CG_PI[0]
                            rel["vto"][ko] = ACT.dma(lambda e, obuf=obuf, r0=r0, g=g: e.dma_start(
                                out=self.pi_d[r0:r0 + 128, g * 512:(g + 1) * 512], in_=obuf[:]), s_stv[ko % 3])

            emit_norm(0)
            emit_transposes(0)
            for ti in range(NT):
                self.feed2(l, 3)
                ck = load_cs(ti)
                if ti + 1 < NT:
                    emit_norm(ti + 1)
                emit_qk(ti, CG_Q)
                if ti + 1 < NT:
                    emit_transposes(ti + 1)
                emit_qk(ti, CG_K, cs_k=ck)
                emit_v(ti, CG_V + CG_PI)
            for ti in range(max(0, NT - 2), NT):
                ACT.wait(hT_st[ti])
            for ko in range(max(0, cnt["vto"] - 3), cnt["vto"]):
                ACT.wait(rel["vto"][ko])
            for ko in range(max(0, cnt["qko"] - 2), cnt["qko"]):
                POOL.wait(rel["qko"][ko])
            self.emit_block()

    def passA(self, l):
        nc, S = self.nc, self.S
        PE, ACT, DVE, POOL, SP = self.new_queues()
        sem = self.sem
        plan = attn_batches(S)
        SCALE = 1.0 / math.sqrt(128.0)
        with ExitStack() as esA:
            def sb(name, shape, dt):
                return esA.enter_context(nc.sbuf_tensor(f"{name}_L{l}", list(shape), dt))

            def psb(name):
                return esA.enter_context(nc.psum_tensor(f"{name}_L{l}", [128, 512], F32))

            NMV = self.NMV
            masks = sb("pa_masks", [128, NMV, 512], BF16)
            ones = sb("pa_ones", [128, 128], BF16)
            idf = sb("pa_idf", [128, 128], F32)
            idb = sb("pa_idb", [128, 128], BF16)
            QT = [sb(f"pa_QT{i}", [128, S], BF16) for i in range(2)]
            KT = [sb(f"pa_KT{i}", [128, S + 2 * PADV], BF16) for i in range(2)]
            nblk_max = max(dil * (S // dil // 128 + 1) for (_, dil) in GROUPS)
            VB = [sb(f"pa_V{i}", [128, nblk_max, 128], BF16) for i in range(2)]
            PT = [sb(f"pa_P{i}", [128, 2, 512], BF16) for i in range(3)]
            U = [sb(f"pa_U{i}", [128, S], F32) for i in range(4)]
            LsB = [sb(f"pa_Ls{i}", [128, S], F32) for i in range(2)]
            bo = [sb(f"pa_bo{i}", [128, S], BF16) for i in range(2)]
            stb = [psb(f"pa_st{i}") for i in range(4)]
            otb = [psb(f"pa_ot{i}") for i in range(2)]
            lbb = [psb(f"pa_lb{i}") for i in range(2)]

            s_cst = sem("s_cst")
            s_ldq = self.dma_ring("ld_x", 2)
            s_ldk = self.dma_ring("ld_w", 2)
            s_ldv = self.dma_ring("ld_c", 2)
            s_stb = self.dma_ring("st_h", 2)

            SP.dma(lambda e: e.dma_start(out=idf[:], in_=self.ident_in[:, :]), s_cst)
            ev_m = SP.dma(lambda e: e.dma_start(out=masks[:], in_=self.mask_in.rearrange("v p c -> p v c")), s_cst)
            DVE.wait(ev_m)
            ev_id = DVE.sig(lambda e: e.tensor_copy(out=idb[:], in_=idf[:]))
            POOL.op(lambda e: e.memset(ones[:], 1.0))
            for i in range(2):
                POOL.op(lambda e, i=i: e.memset(KT[i][:, 0:PADV], 0.0))
                ev_pad = POOL.sig(lambda e, i=i: e.memset(KT[i][:, PADV + S:PADV + S + PADV], 0.0))

            heads = [(hh, g) for hh in range(4) for g in range(3)]
            cnt = dict(st=0, p=0, ot=0, bo=0)
            rel = dict(st={}, p={}, ot={}, lb={}, bo={})
            head_rel = {}
            ld_ev = {}

            def load_head(hi):
                hh, g = heads[hi]
                head = 4 * g + hh
                dil = GROUPS[g][1]
                L = S // dil
                nm = L // 128 + 1
                slot = hi % 2
                if hi >= 2:
                    SP.wait(head_rel[hi - 2])
                e1 = SP.dma(lambda e: e.dma_start(out=QT[slot][:, :], in_=self.qT_d[head]), s_ldq[slot])
                e2 = SP.dma(lambda e: e.dma_start(out=KT[slot][:, PADV:PADV + S], in_=self.kT_d[head]), s_ldk[slot])
                e3 = None
                for c in range(dil):
                    r0 = PADV - 64 * dil + c
                    nrows = nm * 128
                    src = self.v_d[r0:r0 + dil * (nrows - 1) + 1:dil, head * 128:(head + 1) * 128]
                    e3 = SP.dma(lambda e, c=c, src=src: e.dma_start(
                        out=VB[slot][:, c * nm:(c + 1) * nm, :], in_=src.rearrange("(m p) d -> p m d", p=128)), s_ldv[slot])
                ld_ev[hi] = (e1, e2, e3)

            flat = []
            for hi, (hh, g) in enumerate(heads):
                for bi, batch in enumerate(plan[g]):
                    flat.append((hi, hh, g, bi, batch))
            sc_state = {}
            u_last = {}
            u_rd, ls_rd = {}, {}
            ls_last = [None]

            def emit_scores(idx):
                hi, hh, g, bi, batch = flat[idx]
                dil = GROUPS[g][1]
                slot = hi % 2
                qt, kt = QT[slot], KT[slot]
                nit = len(batch)
                ks = cnt["st"]; cnt["st"] += 1
                stA, stB = stb[2 * (ks % 2)], stb[2 * (ks % 2) + 1]
                if ks >= 2:
                    PE.wait(rel["st"][ks - 2])
                if bi == 0:
                    for ev in ld_ev[hi]:
                        PE.wait(ev)
                    PE.wait(ev_pad)
                mA, mB = self.midx[g][bi]
                W = nit * 128
                PE.wait(ev_m)
                PE.wait(ev_id)
                for (bank, mX) in ((stA, mA), (stB, mB)):
                    PE.op(lambda e, bank=bank, mX=mX, W=W: e.matmul(
                        bank[:, 0:W], lhsT=idb[:, :], rhs=masks[:, mX, 0:W], start=True, stop=False))
                for i, (c, qb) in enumerate(batch):
                    kA = PADV + dil * (128 * qb - 64) + c
                    kB = kA + 128 * dil
                    q0 = dil * 128 * qb + c
                    qs = qt[:, q0:q0 + dil * 127 + 1:dil]
                    for (bank, k0) in ((stA, kA), (stB, kB)):
                        fn = lambda e, bank=bank, k0=k0, i=i, qs=qs, kt=kt, dil=dil, last=(i == nit - 1): e.matmul(
                            bank[:, i * 128:(i + 1) * 128], lhsT=kt[:, k0:k0 + dil * 127 + 1:dil], rhs=qs,
                            start=False, stop=last)
                        if i == nit - 1 and bank is stB:
                            ev_s = PE.sig(fn)
                        else:
                            PE.op(fn)
                kp = cnt["p"]; cnt["p"] += 1
                pbuf = PT[kp % 3]
                if kp >= 3:
                    ACT.wait(rel["p"][kp - 3])
                ACT.wait(ev_s)
                ACT.op(lambda e, pbuf=pbuf, stA=stA, W=W: e.activation(
                    out=pbuf[:, 0, 0:W], in_=stA[:, 0:W], func=AF.Exp, scale=SCALE))
                ev_e = ACT.sig(lambda e, pbuf=pbuf, stB=stB, W=W: e.activation(
                    out=pbuf[:, 1, 0:W], in_=stB[:, 0:W], func=AF.Exp, scale=SCALE))
                rel["st"][ks] = ev_e
                ev_p = ev_e
                ev_pA = ev_e
                sc_state[idx] = (kp, pbuf, ev_p, ev_pA, W)

            def emit_rest(idx):
                hi, hh, g, bi, batch = flat[idx]
                head = 4 * g + hh
                dil = GROUPS[g][1]
                L = S // dil
                nbc = L // 128
                nm = nbc + 1
                slot = hi % 2
                ui = (3 * hh + g) % 4
                vbuf, ug, Ls = VB[slot], U[ui], LsB[hh % 2]
                mA, mB = self.midx[g][bi]
                nit = len(batch)
                kp, pbuf, ev_p, ev_pA, W = sc_state.pop(idx)
                if bi == 0 and hi >= 1 and hi + 1 < len(heads):
                    load_head(hi + 1)
                ko = cnt["ot"]; cnt["ot"] += 1
                ob, lb = otb[ko % 2], lbb[ko % 2]
                if ko >= 2:
                    for ev in rel["ot"][ko - 2]:
                        PE.wait(ev)
                PE.wait(ev_p)
                PE.wait(ev_pA)
                for i, (c, qb) in enumerate(batch):
                    blkA = c * nm + qb
                    for (half, blk) in ((0, blkA), (1, blkA + 1)):
                        PE.op(lambda e, ob=ob, i=i, half=half, blk=blk, vbuf=vbuf, pbuf=pbuf: e.matmul(
                            ob[:, i * 128:(i + 1) * 128], lhsT=vbuf[:, blk, :], rhs=pbuf[:, half, i * 128:(i + 1) * 128],
                            start=(half == 0), stop=(half == 1)))
                PE.op(lambda e, lb=lb, pbuf=pbuf, W=W: e.matmul(
                    lb[:, 0:W], lhsT=ones[:, :], rhs=pbuf[:, 0, 0:W], start=True, stop=False))
                ev_o = PE.sig(lambda e, lb=lb, pbuf=pbuf, W=W: e.matmul(
                    lb[:, 0:W], lhsT=ones[:, :], rhs=pbuf[:, 1, 0:W], start=False, stop=True))
                rel["p"][kp] = ev_o
                c0, qb0 = batch[0]
                ncls = len(set(c for c, _ in batch))
                nq = nit // ncls
                ugv = ug[:, :].rearrange("p (j d) -> p d j", d=dil)[:, c0:c0 + ncls, qb0 * 128:(qb0 + nq) * 128]
                lsv = Ls[:, :].rearrange("p (j d) -> p d j", d=dil)[:, c0:c0 + ncls, qb0 * 128:(qb0 + nq) * 128]
                obv = ob[:, 0:W].rearrange("p (c j) -> p c j", c=ncls)
                lbv = lb[:, 0:W].rearrange("p (c j) -> p c j", c=ncls)
                DVE.wait(ev_o)
                ACT.wait(ev_o)
                if bi == 0:
                    ACT.wait(u_rd.get(ui))
                    if g == 0:
                        for ev in ls_rd.get(hh % 2, ()):
                            DVE.wait(ev)
                ev_u = ACT.sig(lambda e, ugv=ugv, obv=obv: e.activation(out=ugv, in_=obv, func=AF.Copy))
                u_last[g] = ev_u
                if g == 0:
                    ev_d = DVE.sig(lambda e, lsv=lsv, lbv=lbv: e.tensor_copy(out=lsv, in_=lbv))
                else:
                    if bi == 0:
                        DVE.wait(ls_last[0])
                    ev_d = DVE.sig(lambda e, lsv=lsv, lbv=lbv: e.tensor_tensor(out=lsv, in0=lbv, in1=lsv, op=ALU.add))
                ls_last[0] = ev_d
                rel["ot"][ko] = (ev_d, ev_u)
                if bi == len(plan[g]) - 1:
                    head_rel[hi] = ev_o
                    if g == 2:
                        ACT.wait(ev_d)
                        ev_ln = ACT.sig(lambda e, Ls=Ls: e.activation(out=Ls[:, :], in_=Ls[:, :], func=AF.Ln))
                        ACT.wait(ev_ln)
                        ev_r = ACT.sig(lambda e, Ls=Ls: e.activation(out=Ls[:, :], in_=Ls[:, :], func=AF.Exp, scale=-1.0))
                        DVE.wait(ev_r)
                        POOL.wait(ev_r)
                        frees = []
                        for gg in range(3):
                            kb = cnt["bo"]; cnt["bo"] += 1
                            bb = bo[kb % 2]
                            eng = POOL if gg == 1 else DVE
                            eng.wait(u_last[gg])
                            if kb >= 2:
                                eng.wait(rel["bo"][kb - 2])
                            ugg = (3 * hh + gg) % 4
                            ev_b = eng.sig(lambda e, ugg=ugg, bb=bb, Ls=Ls: e.tensor_tensor(
                                out=bb[:, :], in0=U[ugg][:, :], in1=Ls[:, :], op=ALU.mult))
                            u_rd[ugg] = ev_b
                            frees.append(ev_b)
                            ACT.wait(ev_b)
                            hd = 4 * gg + hh
                            rel["bo"][kb] = ACT.dma(lambda e, bb=bb, hd=hd: e.dma_start(out=self.bT_d[hd], in_=bb[:, :]), s_stb[kb % 2])
                        ls_rd[hh % 2] = frees

            load_head(0)
            load_head(1)
            emit_scores(0)
            for idx in range(len(flat)):
                if idx % 12 == 0:
                    self.feed2(l, 1)
                if idx + 1 < len(flat):
                    emit_scores(idx + 1)
                emit_rest(idx)
            for kb in range(max(0, cnt["bo"] - 2), cnt["bo"]):
                ACT.wait(rel["bo"][kb])
            self.emit_block()

    def pass2(self, l):
        nc, S, NT, NB = self.nc, self.S, self.NT, self.NB
        PE, ACT, DVE, POOL, SP = self.new_queues()
        sem = self.sem
        x_src = self.x_in if l == 0 else self.x_d
        with ExitStack() as es2:
            def sb(name, shape, dt):
                return es2.enter_context(nc.sbuf_tensor(f"{name}_L{l}", list(shape), dt))

            hTt = sb("p2_hT", [128, KC, TT], BF16)
            BTt = sb("p2_BT", [128, NH, TT], BF16)
            PIw = sb("p2_PI", [128, NB_T + 2, POOLW], BF16)
            slabs = [sb(f"p2_slab{i}", [128, KC, 512], BF16) for i in range(3)]
            ptmp = sb("p2_ptmp", [128, 16, TT], BF16)
            AT = sb("p2_AT", [128, 8, TT], BF16)
            SG = [sb(f"p2_SG{i}", [128, TT], BF16) for i in range(4)]
            M1 = sb("p2_M1", [128, KC, TT], BF16)
            T2 = [sb(f"p2_T2{i}", [128, TT], F32) for i in range(2)]
            MT = sb("p2_MT", [128, KC, TT], BF16)
            GT = sb("p2_GT", [128, D], F32)
            XP = [sb(f"p2_XP{i}", [128, 512], F32) for i in range(4)]
            T3 = [sb(f"p2_T3{i}", [128, 512], F32) for i in range(2)]
            Wg = sb("p2_Wg", [128, 4, 2, 256], BF16)
            Bm = sb("p2_Bm", [128, 48, 128], BF16)
            psc = sb("p2_psc", [128, 8], F32)
            banks = [es2.enter_context(nc.psum_tensor(f"p2_b{i}_L{l}", [128, 512], F32)) for i in range(8)]

            s_cst = sem("s_cst")
            s_ldw = self.dma_ring("ld_w", 3)
            s_ldh, s_ldp, s_ldb = sem("ld_h2"), sem("ld_p2"), sem("ld_b2")
            s_ldx = self.dma_ring("ld_x", 4)
            s_stx = self.dma_ring("st_x", 4)

            self.feed_cast(l, 1000)
            self.wait_casts(SP, l)
            SP.dma(lambda e: e.dma_start(out=Wg[:], in_=self.wg_bf[l]), s_cst)
            SP.dma(lambda e: e.dma_start(out=Bm[:], in_=self.pmat_in.rearrange("s w r p t -> p (s w r) t")), s_cst)
            SP.dma(lambda e: e.dma_start(out=psc[:], in_=self.psc_in[l].rearrange("(oc p) -> p oc", p=128),
                                         allow_slow_non_contiguous=True), s_cst)
            ev_cst = SP.dma(lambda e: e.dma_start(out=GT[:], in_=self.mod_d[l, 2].partition_broadcast(128)), s_cst)


            cnt = dict(slab=0, bank=0, sg=0, t2=0, xp=0, t3=0)
            rel = dict(slab={}, bank={}, sg={}, t2={}, xp={}, t3={})
            ld = {}
            tile_ev = {}

            def load_tile(ti):
                Q = ACT if ti > 0 else SP
                prev = tile_ev.get(ti - 1, {})
                b0 = ti * NB_T
                Q.wait(prev.get("hT_done"))
                e_h = Q.dma(lambda e: e.dma_start(
                    out=hTt[:, :, :], in_=self.hT_d[:, :, ti * TT:(ti + 1) * TT].rearrange("kc p t -> p kc t")), s_ldh)
                Q.wait(prev.get("PI_done"))
                w0 = 1 if b0 == 0 else 0
                w1 = NB_T + 1 if b0 + NB_T == NB else NB_T + 2
                r0 = (b0 - 1 + w0) * 128
                e_p = Q.dma(lambda e: e.dma_start(
                    out=PIw[:, w0:w1, :], in_=self.pi_d[r0:r0 + (w1 - w0) * 128, :].rearrange("(w p) c -> p w c", p=128)), s_ldp)
                Q.wait(prev.get("BT_done"))
                e_b = Q.dma(lambda e: e.dma_start(
                    out=BTt[:, :, :], in_=self.bT_d[:, :, ti * TT:(ti + 1) * TT].rearrange("h p t -> p h t")), s_ldb)
                ld[ti] = (e_h, e_p, e_b)

            def load_slab(src_ap, kcn):
                k = cnt["slab"]; cnt["slab"] += 1
                buf = slabs[k % 3]
                if k >= 3:
                    SP.wait(rel["slab"][k - 3])
                ev = SP.dma(lambda e, buf=buf, src_ap=src_ap, kcn=kcn: e.dma_start(out=buf[:, 0:kcn, :], in_=src_ap), s_ldw[k % 3])
                return k, buf, ev

            def get_bank():
                k = cnt["bank"]; cnt["bank"] += 1
                if k >= 8:
                    PE.wait(rel["bank"][k - 8])
                return k, banks[k % 8]

            def mm_group(bank, pairs, waits=()):
                for w in waits:
                    PE.wait(w)
                n = len(pairs)
                ev = None
                for i, (lhsT, rhs) in enumerate(pairs):
                    fn = lambda e, lhsT=lhsT, rhs=rhs, i=i: e.matmul(bank[:, :], lhsT=lhsT, rhs=rhs, start=(i == 0), stop=(i == n - 1))
                    if i == n - 1:
                        ev = PE.sig(fn)
                    else:
                        PE.op(fn)
                return ev

            def get_sg():
                k = cnt["sg"]; cnt["sg"] += 1
                if k >= 4:
                    ACT.wait(rel["sg"][k - 4])
                return k, SG[k % 4]

            SGP = lambda oc: ptmp[:, oc, :]
            PTs = lambda pc: ptmp[:, 8 + pc, :]
            BG = lambda j: ptmp[:, j, :]

            load_tile(0)
            for ti in range(NT):
                self.feed_cast(l + 1, 2)
                b0 = ti * NB_T
                e_h, e_p, e_b = ld[ti]
                tev = tile_ev.setdefault(ti, {})
                prev = tile_ev.get(ti - 1, {})
                for cg in CG_PG:
                    ks, slab, evsl = load_slab(self.wi_bf[l, cg], KC)
                    for ch in range(4):
                        oc = (cg - CG_PG[0]) * 4 + ch
                        kb, bank = get_bank()
                        ev = mm_group(bank, [(slab[:, kc, ch * 128:(ch + 1) * 128], hTt[:, kc, :]) for kc in range(KC)],
                                      waits=(evsl, e_h))
                        ACT.wait(ev)
                        ACT.wait(prev.get("ptmp_done"))
                        rel["bank"][kb] = ACT.sig(lambda e, bank=bank, oc=oc: e.activation(out=SGP(oc), in_=bank[:, :], func=AF.Silu))
                    rel["slab"][ks] = ev
                ev_sgp = rel["bank"][kb]
                ev_pt = None
                for pc in range(8):
                    g = pc // 2
                    kb, bank = get_bank()
                    PE.wait(e_p)
                    PE.wait(ev_cst)
                    for blk in range(NB_T):
                        b = b0 + blk
                        st = 0 if b == 0 else (3 if b == NB - 1 else (2 if b == NB // 2 - 1 else 1))
                        rels = [r for r in (0, 1, 2) if not ((r == 0 and b == 0) or (r == 2 and b == NB - 1))]
                        for ri, r in enumerate(rels):
                            fn = lambda e, bank=bank, blk=blk, r=r, pc=pc, st=st, g=g, ri=ri, nr=len(rels): e.matmul(
                                bank[:, blk * 128:(blk + 1) * 128], lhsT=PIw[:, blk + r, pc * 128:(pc + 1) * 128],
                                rhs=Bm[:, (st * 4 + g) * 3 + r, :], start=(ri == 0), stop=(ri == nr - 1))
                            if blk == NB_T - 1 and ri == len(rels) - 1:
                                ev = PE.sig(fn)
                            else:
                                PE.op(fn)
                    DVE.wait(ev)
                    DVE.wait(prev.get("ptmp_done"))
                    ev_pt = DVE.sig(lambda e, bank=bank, pc=pc: e.tensor_copy(out=PTs(pc), in_=bank[:, :]))
                    rel["bank"][kb] = ev_pt
                tev["PI_done"] = ev
                for oc in range(8):
                    g = oc // 2
                    kb, bank = get_bank()
                    ev = mm_group(bank, [(Wg[:, g, k2, (oc % 2) * 128:(oc % 2 + 1) * 128], PTs(2 * g + k2)) for k2 in range(2)],
                                  waits=(ev_pt, ev_cst))
                    DVE.wait(ev)
                    DVE.wait(ev_sgp)
                    DVE.wait(prev.get("AT_done"))
                    ev_at = DVE.sig(lambda e, bank=bank, oc=oc: e.scalar_tensor_tensor(
                        out=AT[:, oc, :], in0=bank[:, :], scalar=psc[:, oc:oc + 1], in1=SGP(oc), op0=ALU.mult, op1=ALU.mult))
                    rel["bank"][kb] = ev_at
                ev_p3_pe = ev
                for og in range(4):
                    ks2, s2, ev2 = load_slab(self.wi_bf[l, CG_GP[og]], KC)
                    ks1, s1, ev1 = load_slab(self.wpp_bf[l, og], 8)
                    for ch in range(4):
                        f = og * 4 + ch
                        kbG, bankG = get_bank()
                        evG = mm_group(bankG, [(s2[:, kc, ch * 128:(ch + 1) * 128], hTt[:, kc, :]) for kc in range(KC)],
                                       waits=(ev2, e_h))
                        kbA, bankA = get_bank()
                        evA = mm_group(bankA, [(s1[:, kc, ch * 128:(ch + 1) * 128], AT[:, kc, :]) for kc in range(8)],
                                       waits=(ev1, ev_at))
                        ksg, sg = get_sg()
                        ACT.wait(evG)
                        ev_sg = ACT.sig(lambda e, bankG=bankG, sg=sg: e.activation(out=sg[:, :], in_=bankG[:, :], func=AF.Sigmoid))
                        rel["bank"][kbG] = ev_sg
                        DVE.wait(evA)
                        DVE.wait(ev_sg)
                        DVE.wait(prev.get("M1_done"))
                        ev_m1 = DVE.sig(lambda e, bankA=bankA, sg=sg, f=f: e.tensor_tensor(
                            out=M1[:, f, :], in0=bankA[:, :], in1=sg[:, :], op=ALU.mult))
                        rel["bank"][kbA] = ev_m1
                        rel["sg"][ksg] = ev_m1
                    rel["slab"][ks1] = evA
                    rel["slab"][ks2] = evG
                tev["AT_done"] = evA
                ev_bg = None
                for cg in CG_AG:
                    ks, slab, evsl = load_slab(self.wi_bf[l, cg], KC)
                    for ch in range(4):
                        j = (cg - CG_AG[0]) * 4 + ch
                        kb, bank = get_bank()
                        ev = mm_group(bank, [(slab[:, kc, ch * 128:(ch + 1) * 128], hTt[:, kc, :]) for kc in range(KC)],
                                      waits=(evsl, e_h))
                        ksg, sg = get_sg()
                        ACT.wait(ev)
                        ev_sg = ACT.sig(lambda e, bank=bank, sg=sg: e.activation(out=sg[:, :], in_=bank[:, :], func=AF.Silu))
                        rel["bank"][kb] = ev_sg
                        POOL.wait(ev_sg)
                        POOL.wait(e_b)
                        POOL.wait(ev_at)
                        POOL.wait(ev_p3_pe)
                        POOL.wait(prev.get("BG_done"))
                        ev_bg = POOL.sig(lambda e, sg=sg, j=j: e.tensor_tensor(out=BG(j), in0=sg[:, :], in1=BTt[:, j, :], op=ALU.mult))
                        rel["sg"][ksg] = ev_bg
                    rel["slab"][ks] = ev
                tev["BT_done"] = ev_bg
                for og in range(4):
                    ks2, s2, ev2 = load_slab(self.wi_bf[l, CG_GA[og]], KC)
                    ks1, s1, ev1 = load_slab(self.wpa_bf[l, og], NH)
                    for ch in range(4):
                        f = og * 4 + ch
                        kbG, bankG = get_bank()
                        evG = mm_group(bankG, [(s2[:, kc, ch * 128:(ch + 1) * 128], hTt[:, kc, :]) for kc in range(KC)],
                                       waits=(ev2, e_h))
                        kbB, bankB = get_bank()
                        evB = mm_group(bankB, [(s1[:, kc, ch * 128:(ch + 1) * 128], BG(kc)) for kc in range(NH)],
                                       waits=(ev1, ev_bg))
                        ksg, sg = get_sg()
                        ACT.wait(evG)
                        ev_sg = ACT.sig(lambda e, bankG=bankG, sg=sg: e.activation(out=sg[:, :], in_=bankG[:, :], func=AF.Sigmoid))
                        rel["bank"][kbG] = ev_sg
                        kt = cnt["t2"]; cnt["t2"] += 1
                        t2 = T2[kt % 2]
                        if kt >= 2:
                            DVE.wait(rel["t2"][kt - 2])
                        DVE.wait(evB)
                        DVE.wait(ev_sg)
                        ev_t2 = DVE.sig(lambda e, bankB=bankB, sg=sg, t2=t2: e.tensor_tensor(
                            out=t2[:, :], in0=bankB[:, :], in1=sg[:, :], op=ALU.mult))
                        rel["bank"][kbB] = ev_t2
                        rel["sg"][ksg] = ev_t2
                        POOL.wait(ev_t2)
                        POOL.wait(ev_m1)
                        POOL.wait(prev.get("MT_done"))
                        ev_mt = POOL.sig(lambda e, t2=t2, f=f: e.tensor_tensor(out=MT[:, f, :], in0=t2[:, :], in1=M1[:, f, :], op=ALU.add))
                        rel["t2"][kt] = ev_mt
                    rel["slab"][ks1] = evB
                    rel["slab"][ks2] = evG
                tev["hT_done"] = evB
                tev["BG_done"] = evB
                tev["ptmp_done"] = evB
                tev["M1_done"] = ev_mt
                if ti + 1 < NT:
                    load_tile(ti + 1)
                for og in range(4):
                    ks, slab, evsl = load_slab(self.wo_bf[l, og], KC)
                    for blk in range(NB_T):
                        r0 = (b0 + blk) * 128
                        kx = cnt["xp"]; cnt["xp"] += 1
                        xp = XP[kx % 4]
                        if kx >= 4:
                            SP.wait(rel["xp"][kx - 4])
                        ev_x = SP.dma(lambda e, xp=xp, r0=r0, og=og: e.dma_start(
                            out=xp[:, :], in_=x_src[r0:r0 + 128, og * 512:(og + 1) * 512]), s_ldx[kx % 4])
                        kb, bank = get_bank()
                        ev = mm_group(bank, [(MT[:, kc, blk * 128:(blk + 1) * 128], slab[:, kc, :]) for kc in range(KC)],
                                      waits=(evsl, ev_mt))
                        k3 = cnt["t3"]; cnt["t3"] += 1
                        t3 = T3[k3 % 2]
                        if k3 >= 2:
                            DVE.wait(rel["t3"][k3 - 2])
                        DVE.wait(ev)
                        DVE.wait(ev_cst)
                        ev_t3 = DVE.sig(lambda e, bank=bank, t3=t3, og=og: e.tensor_tensor(
                            out=t3[:, :], in0=bank[:, :], in1=GT[:, og * 512:(og + 1) * 512], op=ALU.mult))
                        rel["bank"][kb] = ev_t3
                        POOL.wait(ev_t3)
                        POOL.wait(ev_x)
                        ev_xn = POOL.sig(lambda e, t3=t3, xp=xp: e.tensor_tensor(out=xp[:, :], in0=t3[:, :], in1=xp[:, :], op=ALU.add))
                        rel["t3"][k3] = ev_xn
                        POOL.wait(ev_xn)
                        rel["xp"][kx] = POOL.dma(lambda e, xp=xp, r0=r0, og=og: e.dma_start(
                            out=self.x_d[r0:r0 + 128, og * 512:(og + 1) * 512], in_=xp[:, :]), s_stx[kx % 4])
                    rel["slab"][ks] = ev
                tev["MT_done"] = ev
            for kx in range(max(0, cnt["xp"] - 4), cnt["xp"]):
                POOL.wait(rel["xp"][kx])
            self.emit_block()

    def passF(self):
        nc, S, NB = self.nc, self.S, self.NB
        PE, ACT, DVE, POOL, SP = self.new_queues()
        sem = self.sem
        with ExitStack() as esf:
            def sb(name, shape, dt):
                return esf.enter_context(nc.sbuf_tensor(name, list(shape), dt))
            FG = sb("pf_FG", [128, D], F32)
            eps_t = sb("pf_eps", [128, 1], F32)
            xb = [sb(f"pf_x{i}", [128, D], F32) for i in range(4)]
            yb = [sb(f"pf_y{i}", [128, D], F32) for i in range(3)]
            junk = sb("pf_junk", [128, D], BF16)
            stat = sb("pf_stat", [128, 32], F32)
            s_cst = sem("s_cst")
            s_ldx = self.dma_ring("ld_x", 4)
            s_sty = self.dma_ring("st_y", 3)
            ev_cst = SP.dma(lambda e: e.dma_start(out=FG[:], in_=self.fg_in.partition_broadcast(128)), s_cst)
            ev_eps = POOL.sig(lambda e: e.memset(eps_t[:], EPS))
            ACT.wait(ev_eps)
            relx, rely = {}, {}
            for b in range(NB):
                xbuf, ybuf = xb[b % 4], yb[b % 3]
                if b >= 4:
                    SP.wait(relx[b - 4])
                evx = SP.dma(lambda e, xbuf=xbuf, b=b: e.dma_start(out=xbuf[:], in_=self.x_d[b * 128:(b + 1) * 128, :]), s_ldx[b % 4])
                col = (b % 8) * 3
                ACT.wait(evx)
                eva = ACT.sig(lambda e, xbuf=xbuf, col=col: e.activation(
                    out=junk[:], in_=xbuf[:], func=AF.Square, accum_out=stat[:, col:col + 1]))
                ACT.wait(eva)
                evs = ACT.sig(lambda e, col=col: e.activation(
                    out=stat[:, col + 1:col + 2], in_=stat[:, col:col + 1], func=AF.Sqrt, bias=eps_t[:, 0:1], scale=1.0 / D))
                DVE.wait(evs)
                evd = DVE.sig(lambda e, col=col: e.reciprocal(out=stat[:, col + 2:col + 3], in_=stat[:, col + 1:col + 2]))
                DVE.wait(evd)
                DVE.wait(ev_cst)
                if b >= 3:
                    DVE.wait(rely[b - 3])
                evy = DVE.sig(lambda e, xbuf=xbuf, ybuf=ybuf, col=col: e.scalar_tensor_tensor(
                    out=ybuf[:], in0=xbuf[:], scalar=stat[:, col + 2:col + 3], in1=FG[:], op0=ALU.mult, op1=ALU.mult))
                relx[b] = evy
                ACT.wait(evy)
                rely[b] = ACT.dma(lambda e, ybuf=ybuf, b=b: e.dma_start(out=self.y_out[b * 128:(b + 1) * 128, :], in_=ybuf[:]), s_sty[b % 3])
            for b in range(max(0, NB - 3), NB):
                ACT.wait(rely[b])
            self.emit_block()

    def finish(self):
        self.es.close()
        return self.nc


def core_inputs(w, x, c, S, s_real):
    xs = np.zeros((S, D), np.float32)
    xs[:x.shape[0]] = x
    perm = qk_perm()
    cols = np.arange(INW)
    cols[2048:2048 + ATTW] = 2048 + perm
    cols[2048 + ATTW:2048 + 2 * ATTW] = 2048 + ATTW + perm
    w_in_p = w["w_in_p"] if "w_in_p" in w else np.ascontiguousarray(w["w_in"][:, :, cols])
    cos, sin = rope_tables(S)
    return {
        "x": xs, "c": np.ascontiguousarray(c, dtype=np.float32),
        "norm_gain": w["norm_gain"], "w_ada": w["w_ada"], "b_ada": w["b_ada"], "w_in": w_in_p,
        "w_pool_grp": w["w_pool_grp"], "pool_scale": w["pool_scale"], "w_proj_pool": w["w_proj_pool"],
        "w_proj_attn": w["w_proj_attn"], "w_out": w["w_out"], "final_gain": w["final_gain"],
        "rope_cos": cos, "rope_sin": sin, "pool_mats": pool_mats(S, s_real), "attn_masks": mask_table(S, s_real),
        "ident": np.eye(128, dtype=np.float32),
        "tile_valid": np.ascontiguousarray(np.broadcast_to(
            ((np.arange(S // TT) * TT) < s_real).astype(np.float32)[None, :], (128, S // TT))),
    }


WEIGHT_NAMES = ("norm_gain", "w_ada", "b_ada", "w_in", "w_pool_grp", "pool_scale", "w_proj_pool", "w_proj_attn",
                "w_out", "final_gain")


def build_program(S=4096, depth=4):
    P = Prog(S=S, depth=depth)
    P.phase0()
    for l in range(depth):
        P.pass1(l)
        P.passA(l)
        P.pass2(l)
    P.passF()
    return P.finish()


def kernel(x_prompt, x_sample, c_prompt, c_sample, norm_gain, w_ada, b_ada, w_in, w_pool_grp, pool_scale,
           w_proj_pool, w_proj_attn, w_out, final_gain):
    S = 4096
    depth = int(np.shape(w_in)[0])
    loc = dict(norm_gain=norm_gain, w_ada=w_ada, b_ada=b_ada, w_in=w_in, w_pool_grp=w_pool_grp, pool_scale=pool_scale,
               w_proj_pool=w_proj_pool, w_proj_attn=w_proj_attn, w_out=w_out, final_gain=final_gain)
    w = {k: np.ascontiguousarray(np.asarray(v), dtype=np.float32) for k, v in loc.items()}
    perm = qk_perm()
    cols = np.arange(INW)
    cols[2048:2048 + ATTW] = 2048 + perm
    cols[2048 + ATTW:2048 + 2 * ATTW] = 2048 + ATTW + perm
    w["w_in_p"] = np.ascontiguousarray(w["w_in"][:, :, cols])
    x_prompt = np.asarray(x_prompt, dtype=np.float32)
    x_sample = np.asarray(x_sample, dtype=np.float32)
    c_prompt = np.asarray(c_prompt, dtype=np.float32)
    c_sample = np.asarray(c_sample, dtype=np.float32)
    nsm, npr = x_sample.shape[0], x_prompt.shape[0]
    order = [("s", 0), ("s", 1), ("p", 0), ("p", 1), ("s", 2), ("s", 3), ("p", 2), ("p", 3)]
    maps = []
    for kind, i in order:
        if kind == "s":
            maps.append(core_inputs(w, x_sample[i], c_sample[i], S, x_sample.shape[1]))
        else:
            maps.append(core_inputs(w, x_prompt[i], c_prompt[i], S, x_prompt.shape[1]))
    nc = build_program(S=S, depth=depth)
    res = run_bass_kernel_spmd(nc, maps, core_ids=list(range(len(maps))))
    ys = [np.asarray(r["y"], dtype=np.float32) for r in res.results]
    y_sample = np.zeros(x_sample.shape, np.float32)
    y_prompt = np.zeros(x_prompt.shape, np.float32)
    for ci, (kind, i) in enumerate(order):
        if kind == "s":
            y_sample[i] = ys[ci][:x_sample.shape[1]]
        else:
            y_prompt[i] = ys[ci][:x_prompt.shape[1]]
    return (y_prompt, y_sample)
```

```python
import math
from contextlib import ExitStack

import numpy as np
import ml_dtypes

import concourse.bass as bass
import concourse.mybir as mybir
from concourse.bass_utils import run_bass_kernel_spmd

F32 = mybir.dt.float32
BF16 = mybir.dt.bfloat16
AF = mybir.ActivationFunctionType
ALU = mybir.AluOpType

D = 2048
KC = 16
INW = 12288
POOLW = 1024
ATTW = 1536
NH = 12
GROUPS = ((128, 1), (512, 4), (2048, 16))
EPS = 1e-6
PADV = 1024
NB_T = 4
TT = NB_T * 128

CG_PI = (0, 1)
CG_PG = (2, 3)
CG_Q = (4, 5, 6)
CG_K = (7, 8, 9)
CG_V = (10, 11, 12)
CG_AG = (13, 14, 15)
CG_GP = (16, 17, 18, 19)
CG_GA = (20, 21, 22, 23)


class Sem:
    def __init__(self, nc, es, name):
        self.h = es.enter_context(nc.semaphore(name))
        self.v = 0
        self.name = name


class EngQ:
    def __init__(self, name):
        self.name = name
        self.ops = []
        self.waited = {}
        self.csem = None

    def wait(self, ev):
        if ev is None:
            return
        sem, val = ev
        if val <= 0 or self.waited.get(sem.name, 0) >= val:
            return
        self.waited[sem.name] = val
        self.ops.append(("w", sem, val))

    def op(self, fn):
        if self.csem is not None and self.name != "pe":
            self.sig(fn)
        else:
            self.ops.append(("i", fn, None, 0))

    def sig(self, fn):
        s = self.csem
        s.v += 1
        self.ops.append(("i", fn, s, 1))
        return (s, s.v)

    def dma(self, fn, sem):
        sem.v += 16
        self.ops.append(("i", fn, sem, 16))
        return (sem, sem.v)

    def run(self, eng):
        for o in self.ops:
            if o[0] == "w":
                eng.wait_ge(o[1].h, o[2])
            else:
                ins = o[1](eng)
                if o[2] is not None:
                    ins.then_inc(o[2].h, o[3])


def qk_perm():
    perm = []
    for pr in range(NH // 2):
        a, b = 2 * pr, 2 * pr + 1
        perm += list(range(a * 128, a * 128 + 64)) + list(range(b * 128, b * 128 + 64))
        perm += list(range(a * 128 + 64, a * 128 + 128)) + list(range(b * 128 + 64, b * 128 + 128))
    return np.array(perm)


def rope_tables(S):
    half = 64
    inv = (10000.0 ** (-np.arange(half, dtype=np.float32) / np.float32(half))).astype(np.float32)
    pos = np.arange(S, dtype=np.float32)
    ang = (pos[None, :] * inv[:, None]).astype(np.float32)
    cos = np.cos(ang).astype(np.float32)
    sin = np.sin(ang).astype(np.float32)
    return np.concatenate([cos, cos], 0), np.concatenate([sin, sin], 0)


def pool_mats(S, s_real):
    wins = (2, 4, 8, 16)
    nb = S // 128
    blocks = (0, 1 if nb > 2 else 0, nb // 2 - 1, nb - 1)
    out = np.zeros((4, 4, 3, 128, 128), np.float32)
    for si, b in enumerate(blocks):
        for wi, w in enumerate(wins):
            h = w // 2
            for tl in range(128):
                t = b * 128 + tl
                if t >= s_real:
                    out[si, wi, 1, tl, tl] = 0.0
                    continue
                lo, hi = max(0, t - h), min(s_real, t + h)
                cnt = hi - lo
                for tp in range(lo, hi):
                    rel = tp // 128 - b + 1
                    out[si, wi, rel, tp % 128, tl] += 1.0 / cnt
                out[si, wi, 1, tl, tl] -= 1.0
    return out.astype(ml_dtypes.bfloat16)


def attn_batches(S):
    plan = []
    for (win, dil) in GROUPS:
        L = S // dil
        nbc = L // 128
        items = [(c, qb) for c in range(dil) for qb in range(nbc)]
        plan.append([items[i:i + 4] for i in range(0, len(items), 4)])
    return plan


def mask_variants(S):
    sigs = []
    idx = []
    for gi, (win, dil) in enumerate(GROUPS):
        L = S // dil
        nbc = L // 128
        gidx = []
        for batch in attn_batches(S)[gi]:
            sa = tuple("first" if qb == 0 else "mid" for (c, qb) in batch)
            sb = tuple("last" if qb == nbc - 1 else ("half" if (2 * (qb + 1) == nbc) else "mid") for (c, qb) in batch)
            pair = []
            for kind, sg in (("A", sa), ("B", sb)):
                key = (kind,) + sg
                if key not in sigs:
                    sigs.append(key)
                pair.append(sigs.index(key))
            gidx.append(tuple(pair))
        idx.append(gidx)
    return idx, sigs


def mask_table(S, s_real):
    _, sigs = mask_variants(S)
    tab = np.zeros((len(sigs), 128, 512), np.float32)
    p = np.arange(128)[:, None]
    c = np.arange(128)[None, :]
    for i, key in enumerate(sigs):
        kind = key[0]
        for j, v in enumerate(key[1:]):
            if kind == "A":
                m = (p >= c).astype(np.float32)
                if v == "first":
                    m = m * (p >= 64)
            else:
                m = (p <= c).astype(np.float32)
                if v == "last":
                    m = m * (p < 64)
                if v == "half" and s_real < S:
                    m = m * (p < 64)
            tab[i, :, j * 128:(j + 1) * 128] = m
    return tab.astype(ml_dtypes.bfloat16)


class Prog:
    def __init__(self, S=4096, depth=4, dbg=False):
        self.S, self.depth, self.dbg = S, depth, dbg
        self.NB = S // 128
        self.NT = S // TT
        self.nc = nc = bass.Bass("TRN2", target_bir_lowering=False)
        scr = "ExternalOutput" if dbg else "Internal"

        def dram(name, shape, dt, kind):
            return nc.dram_tensor(name, list(shape), dt, kind=kind).ap()

        self.x_in = dram("x", [S, D], F32, "ExternalInput")
        self.c_in = dram("c", [D], F32, "ExternalInput")
        self.ng_in = dram("norm_gain", [depth, D], F32, "ExternalInput")
        self.wada_in = dram("w_ada", [depth, D, 3 * D], F32, "ExternalInput")
        self.bada_in = dram("b_ada", [depth, 3 * D], F32, "ExternalInput")
        self.win_in = dram("w_in", [depth, D, INW], F32, "ExternalInput")
        self.wgrp_in = dram("w_pool_grp", [depth, 4, 256, 256], F32, "ExternalInput")
        self.psc_in = dram("pool_scale", [depth, POOLW], F32, "ExternalInput")
        self.wpp_in = dram("w_proj_pool", [depth, POOLW, D], F32, "ExternalInput")
        self.wpa_in = dram("w_proj_attn", [depth, ATTW, D], F32, "ExternalInput")
        self.wout_in = dram("w_out", [depth, D, D], F32, "ExternalInput")
        self.fg_in = dram("final_gain", [D], F32, "ExternalInput")
        self.cos_in = dram("rope_cos", [128, S], F32, "ExternalInput")
        self.sin_in = dram("rope_sin", [128, S], F32, "ExternalInput")
        self.pmat_in = dram("pool_mats", [4, 4, 3, 128, 128], BF16, "ExternalInput")
        self.midx, self.msigs = mask_variants(S)
        self.NMV = len(self.msigs)
        self.mask_in = dram("attn_masks", [self.NMV, 128, 512], BF16, "ExternalInput")
        self.ident_in = dram("ident", [128, 128], F32, "ExternalInput")
        self.tval_in = dram("tile_valid", [128, S // TT], F32, "ExternalInput")
        self.y_out = dram("y", [S, D], F32, "ExternalOutput")

        self.wi_bf = dram("wi_bf", [depth, 24, 128, KC, 512], BF16, "Internal")
        self.wg_bf = dram("wg_bf", [depth, 128, 4, 2, 256], BF16, "Internal")
        self.wpp_bf = dram("wpp_bf", [depth, 4, 128, 8, 512], BF16, "Internal")
        self.wpa_bf = dram("wpa_bf", [depth, 4, 128, 12, 512], BF16, "Internal")
        self.wo_bf = dram("wo_bf", [depth, 4, 128, KC, 512], BF16, "Internal")
        self.mod_d = dram("mod_d", [depth, 3, D], F32, scr)
        self.hT_d = dram("hT_d", [KC, 128, S], BF16, scr)
        self.qT_d = dram("qT_d", [NH, 128, S], BF16, scr)
        self.kT_d = dram("kT_d", [NH, 128, S], BF16, scr)
        self.v_d = dram("v_d", [S + 2 * PADV, ATTW], BF16, scr)
        self.pi_d = dram("pi_d", [S, POOLW], BF16, scr)
        self.bT_d = dram("bT_d", [NH, 128, S], BF16, scr)
        self.x_d = dram("x_d", [S, D], F32, scr)

        self.es = ExitStack()
        self.sems = {}
        self.cast_ev = [None] * depth
        self.cast_pending = {}
        self.cast_evs = {}
        self.cast_slot_last = {}
        self.cast_k = 0

    def sem(self, name):
        if name not in self.sems:
            self.sems[name] = Sem(self.nc, self.es, name)
        return self.sems[name]

    def new_queues(self):
        self.PE, self.ACT, self.DVE, self.POOL, self.SP = EngQ("pe"), EngQ("act"), EngQ("dve"), EngQ("pool"), EngQ("sp")
        for q in (self.PE, self.ACT, self.DVE, self.POOL):
            q.csem = self.sem("c_" + q.name)
        return self.PE, self.ACT, self.DVE, self.POOL, self.SP

    def emit_block(self):
        with self.nc.Block() as block:
            @block.tensor
            def _(e):
                self.PE.run(e)

            @block.scalar
            def _(e):
                self.ACT.run(e)

            @block.vector
            def _(e):
                self.DVE.run(e)

            @block.gpsimd
            def _(e):
                self.POOL.run(e)

            @block.sync
            def _(e):
                self.SP.run(e)

    def cast_list(self, l):
        lst = []
        order = list(CG_Q + CG_K + CG_V + CG_PI) + [cg for cg in range(24) if cg not in (CG_Q + CG_K + CG_V + CG_PI)]
        for cg in order:
            lst.append(lambda e, cg=cg: e.dma_start(
                out=self.wi_bf[l, cg],
                in_=self.win_in[l][:, cg * 512:(cg + 1) * 512].rearrange("(kc p) c -> p kc c", p=128)))
        for g in range(4):
            lst.append(lambda e, g=g: e.dma_start(
                out=self.wg_bf[l][:, g], in_=self.wgrp_in[l, g].rearrange("(kc p) c -> p kc c", p=128)))
        for og in range(4):
            for dst, src in ((self.wpp_bf, self.wpp_in), (self.wpa_bf, self.wpa_in), (self.wo_bf, self.wout_in)):
                lst.append(lambda e, og=og, dst=dst, src=src: e.dma_start(
                    out=dst[l, og], in_=src[l][:, og * 512:(og + 1) * 512].rearrange("(kc p) c -> p kc c", p=128)))
        return lst

    def feed_cast(self, l, n):
        if l >= self.depth:
            return
        if l not in self.cast_pending:
            self.cast_pending[l] = self.cast_list(l)
        evs = self.cast_evs.setdefault(l, {})
        for _ in range(n):
            if not self.cast_pending[l]:
                break
            fn = self.cast_pending[l].pop(0)
            slot = self.cast_k % 3
            self.cast_k += 1
            self.POOL.wait(self.cast_slot_last.get(slot))
            ev = self.POOL.dma(fn, self.sem(f"s_castslot{slot}"))
            self.cast_slot_last[slot] = ev
            evs[slot] = ev

    def wait_casts(self, q, l):
        for ev in self.cast_evs.get(l, {}).values():
            q.wait(ev)

    def feed2(self, l, n):
        for _ in range(n):
            if self.cast_pending.get(l) is None and l < self.depth:
                self.cast_pending[l] = self.cast_list(l)
            if l < self.depth and self.cast_pending[l]:
                self.feed_cast(l, 1)
            else:
                self.feed_cast(l + 1, 1)

    def dma_ring(self, prefix, n):
        return [self.sem(f"{prefix}{i}") for i in range(n)]

    def phase0(self):
        nc, depth, S = self.nc, self.depth, self.S
        PE, ACT, DVE, POOL, SP = self.new_queues()
        sem = self.sem
        with ExitStack() as es0:
            def sb0(name, shape, dt):
                return es0.enter_context(nc.sbuf_tensor(name, list(shape), dt))
            zeros_bf = sb0("zeros_bf", [128, ATTW], BF16)
            cT = sb0("cT", [128, KC], F32)
            slabs = [sb0(f"ada_slab{i}", [128, 3072], F32) for i in range(3)]
            modrow = sb0("modrow", [1, 3 * D], F32)
            badarow = sb0("badarow", [1, 3 * D], F32)
            gainrow = sb0("gainrow", [1, D], F32)
            acc = [es0.enter_context(nc.psum_tensor(f"ada_acc{j}", [128, 512], F32)) for j in range(6)]

            s_vpad = sem("s_vpad")
            ez = POOL.sig(lambda e: e.memset(zeros_bf[:], 0.0))
            POOL.wait(ez)
            for r0 in list(range(0, PADV, 128)) + list(range(PADV + S, PADV + S + PADV, 128)):
                evp = POOL.dma(lambda e, r0=r0: e.dma_start(out=self.v_d[r0:r0 + 128, :], in_=zeros_bf[:, :]), s_vpad)
            self.feed_cast(0, 11)
            POOL.wait(evp)

            s_c, s_row, s_st = sem("s0_c"), sem("s0_row"), sem("s0_st")
            s_ld = self.dma_ring("ld_a", 3)
            evc = SP.dma(lambda e: e.dma_start(out=cT[:], in_=self.c_in.rearrange("(kc p) -> p kc", p=128),
                                               allow_slow_non_contiguous=True), s_c)
            slab_rel = {}
            nslab = 0
            ev_st = None
            ev_acc_free = None
            for l in range(depth):
                if ev_st is not None:
                    SP.wait(ev_st)
                SP.dma(lambda e, l=l: e.dma_start(out=badarow[0:1, :], in_=self.bada_in[l:l + 1, :]), s_row)
                ev_row = SP.dma(lambda e, l=l: e.dma_start(out=gainrow[0:1, :], in_=self.ng_in[l:l + 1, :]), s_row)
                ev_last = None
                for hf in range(2):
                    ev_acc = []
                    for kc in range(KC):
                        k = nslab
                        nslab += 1
                        buf = slabs[k % 3]
                        if k >= 3:
                            SP.wait(slab_rel[k - 3])
                        ev_ld = SP.dma(lambda e, l=l, hf=hf, kc=kc, buf=buf: e.dma_start(
                            out=buf[:], in_=self.wada_in[l][kc * 128:(kc + 1) * 128, hf * 3072:(hf + 1) * 3072]), s_ld[k % 3])
                        PE.wait(ev_ld)
                        PE.wait(evc)
                        if kc == 0 and ev_acc_free is not None:
                            PE.wait(ev_acc_free)
                        for j in range(6):
                            fn = lambda e, j=j, kc=kc, buf=buf: e.matmul(
                                acc[j][0:1, :], lhsT=cT[:, kc:kc + 1], rhs=buf[:, j * 512:(j + 1) * 512],
                                start=(kc == 0), stop=(kc == KC - 1))
                            if kc == KC - 1:
                                ev_acc.append(PE.sig(fn))
                                if j == 5:
                                    slab_rel[k] = ev_acc[-1]
                            elif j == 5:
                                slab_rel[k] = PE.sig(fn)
                            else:
                                PE.op(fn)
                    DVE.wait(ev_row)
                    if ev_st is not None:
                        DVE.wait(ev_st)
                    for j in range(6):
                        DVE.wait(ev_acc[j])
                        c0 = hf * 3072 + j * 512
                        ev_last = DVE.sig(lambda e, j=j, c0=c0: e.tensor_tensor(
                            out=modrow[0:1, c0:c0 + 512], in0=acc[j][0:1, :], in1=badarow[0:1, c0:c0 + 512], op=ALU.add))
                    ev_acc_free = ev_last
                DVE.wait(ev_last)
                evg = DVE.sig(lambda e: e.scalar_tensor_tensor(
                    out=modrow[0:1, D:2 * D], in0=modrow[0:1, D:2 * D], scalar=1.0, in1=gainrow[0:1, :],
                    op0=ALU.add, op1=ALU.mult))
                SP.wait(evg)
                SP.dma(lambda e, l=l: e.dma_start(out=self.mod_d[l, 0:1, :], in_=modrow[0:1, D:2 * D]), s_st)
                SP.dma(lambda e, l=l: e.dma_start(out=self.mod_d[l, 1:2, :], in_=modrow[0:1, 0:D]), s_st)
                ev_st = SP.dma(lambda e, l=l: e.dma_start(out=self.mod_d[l, 2:3, :], in_=modrow[0:1, 2 * D:3 * D]), s_st)
            SP.wait(ev_st)
            self.emit_block()

    def pass1(self, l):
        nc, S, NT = self.nc, self.S, self.NT
        PE, ACT, DVE, POOL, SP = self.new_queues()
        sem = self.sem
        x_src = self.x_in if l == 0 else self.x_d
        with ExitStack() as es1:
            def sb(name, shape, dt):
                return es1.enter_context(nc.sbuf_tensor(f"{name}_L{l}", list(shape), dt))

            def psb(name):
                return es1.enter_context(nc.psum_tensor(f"{name}_L{l}", [128, 512], F32))

            ident = sb("p1_ident", [128, 128], F32)
            Gf = sb("p1_Gf", [128, KC], F32)
            SHf = sb("p1_SHf", [128, KC], F32)
            eps_t = sb("p1_eps", [128, 1], F32)
            tval = sb("p1_tval", [128, NT], F32)
            SHt = [sb(f"p1_SHt{i}", [128, KC], F32) for i in range(2)]
            xb = [sb(f"p1_xb{i}", [128, D], F32) for i in range(4)]
            junk = sb("p1_junk", [128, D], BF16)
            stat = sb("p1_stat", [128, 32], F32)
            xn = [sb(f"p1_xn{i}", [128, D], BF16) for i in range(NB_T)]
            identb = sb("p1_identb", [128, 128], BF16)
            hT = [sb(f"p1_hT{i}", [128, KC, TT], BF16) for i in range(2)]
            slabs = [sb(f"p1_slab{i}", [128, KC, 512], BF16) for i in range(3)]
            cs = [sb(f"p1_cs{i}", [128, 2, TT], F32) for i in range(2)]
            rt = [sb(f"p1_rt{i}", [128, 4, TT], F32) for i in range(2)]
            qko = [sb(f"p1_qko{i}", [128, 2, TT], BF16) for i in range(2)]
            vto = [sb(f"p1_vto{i}", [128, 512], BF16) for i in range(3)]
            tpb = [es1.enter_context(nc.psum_tensor(f"p1_tpb{i}_L{l}", [128, 512], BF16)) for i in range(2)]
            qkb = [psb(f"p1_qk{i}") for i in range(4)]
            vb = [psb(f"p1_v{i}") for i in range(2)]

            s_cst = sem("s_cst")
            s_ldx = self.dma_ring("ld_x", 4)
            s_ldw = self.dma_ring("ld_w", 3)
            s_ldc = self.dma_ring("ld_c", 2)
            s_sth = self.dma_ring("st_h", 2)
            s_stq = self.dma_ring("st_q", 2)
            s_stv = self.dma_ring("st_v", 3)

            SP.dma(lambda e: e.dma_start(out=ident[:], in_=self.ident_in[:, :]), s_cst)
            SP.dma(lambda e: e.dma_start(out=tval[:], in_=self.tval_in[:, :]), s_cst)
            SP.dma(lambda e: e.dma_start(out=Gf[:], in_=self.mod_d[l, 0].rearrange("(kc p) -> p kc", p=128),
                                         allow_slow_non_contiguous=True), s_cst)
            ev_cst = SP.dma(lambda e: e.dma_start(out=SHf[:], in_=self.mod_d[l, 1].rearrange("(kc p) -> p kc", p=128),
                                                  allow_slow_non_contiguous=True), s_cst)
            self.wait_casts(SP, l)
            ev_eps = POOL.sig(lambda e: e.memset(eps_t[:], EPS))
            ACT.wait(ev_eps)
            DVE.wait(ev_cst)
            ev_idb = DVE.sig(lambda e: e.tensor_copy(out=identb[:], in_=ident[:]))

            cnt = dict(x=0, tp=0, slab=0, qk=0, rt=0, qko=0, v=0, vto=0, cs=0)
            rel = dict(x={}, tp={}, slab={}, qk={}, rt={}, qko={}, v={}, vto={}, cs={})
            hT_ready, hT_pe_done, hT_st = {}, {}, {}
            xn_ready, xn_free = {}, {}
            cs_buf, cs_ev = {}, {}

            def emit_norm(ti):
                for blk in range(NB_T):
                    gb = ti * NB_T + blk
                    t0 = gb * 128
                    k = cnt["x"]; cnt["x"] += 1
                    buf = xb[k % 4]
                    if k >= 4:
                        SP.wait(rel["x"][k - 4])
                    evx = SP.dma(lambda e, buf=buf, t0=t0: e.dma_start(out=buf[:], in_=x_src[t0:t0 + 128, :]), s_ldx[k % 4])
                    col = (gb % 8) * 3
                    ACT.wait(evx)
                    eva = ACT.sig(lambda e, buf=buf, col=col: e.activation(
                        out=junk[:], in_=buf[:], func=AF.Square, accum_out=stat[:, col:col + 1]))
                    ACT.wait(eva)
                    evs = ACT.sig(lambda e, col=col: e.activation(
                        out=stat[:, col + 1:col + 2], in_=stat[:, col:col + 1], func=AF.Sqrt, bias=eps_t[:, 0:1], scale=1.0 / D))
                    DVE.wait(evs)
                    evd = DVE.sig(lambda e, col=col: e.reciprocal(out=stat[:, col + 2:col + 3], in_=stat[:, col + 1:col + 2]))
                    DVE.wait(evd)
                    if ti > 0:
                        DVE.wait(xn_free[ti - 1])
                    evxn = DVE.sig(lambda e, buf=buf, blk=blk, col=col: e.tensor_scalar(
                        out=xn[blk][:], in0=buf[:], scalar1=stat[:, col + 2:col + 3], scalar2=None, op0=ALU.mult))
                    rel["x"][k] = evxn
                    xn_ready[ti] = evxn

            def emit_transposes(ti):
                hbuf = hT[ti % 2]
                evh = None
                for kc in range(KC):
                    k = cnt["tp"]; cnt["tp"] += 1
                    bank = tpb[k % 2]
                    if k >= 2:
                        PE.wait(rel["tp"][k - 2])
                    if kc == 0:
                        PE.wait(xn_ready[ti])
                        PE.wait(ev_cst)
                        PE.wait(ev_idb)
                    for blk in range(NB_T):
                        fn = lambda e, bank=bank, blk=blk, kc=kc: e.transpose(
                            bank[:, blk * 128:(blk + 1) * 128], xn[blk][:, kc * 128:(kc + 1) * 128], identb[:])
                        if blk == NB_T - 1:
                            evt = PE.sig(fn)
                        else:
                            PE.op(fn)
                    DVE.wait(evt)
                    if kc == 0:
                        DVE.wait(ev_cst)
                        if ti >= 2:
                            DVE.wait(hT_pe_done[ti - 2])
                            DVE.wait(hT_st[ti - 2])
                        sht = SHt[ti % 2]
                        ev_sh = DVE.sig(lambda e, sht=sht, ti=ti: e.tensor_scalar(
                            out=sht[:, :], in0=SHf[:, :], scalar1=tval[:, ti:ti + 1], scalar2=None, op0=ALU.mult))
                        DVE.wait(ev_sh)
                    evh = DVE.sig(lambda e, bank=bank, kc=kc, hbuf=hbuf, sht=sht: e.tensor_scalar(
                        out=hbuf[:, kc, :], in0=bank[:, :], scalar1=Gf[:, kc:kc + 1], scalar2=sht[:, kc:kc + 1],
                        op0=ALU.mult, op1=ALU.add))
                    rel["tp"][k] = evh
                xn_free[ti] = evt
                hT_ready[ti] = evh
                ACT.wait(evh)
                hT_st[ti] = ACT.dma(lambda e, hbuf=hbuf, ti=ti: e.dma_start(
                    out=self.hT_d[:, :, ti * TT:(ti + 1) * TT].rearrange("kc p t -> p kc t"), in_=hbuf[:, :, :]), s_sth[ti % 2])

            def load_slab(cg):
                k = cnt["slab"]; cnt["slab"] += 1
                buf = slabs[k % 3]
                if k >= 3:
                    SP.wait(rel["slab"][k - 3])
                ev = SP.dma(lambda e, buf=buf, cg=cg: e.dma_start(out=buf[:], in_=self.wi_bf[l, cg]), s_ldw[k % 3])
                return k, buf, ev

            def load_cs(ti):
                k = cnt["cs"]; cnt["cs"] += 1
                buf = cs[k % 2]
                if k >= 2:
                    SP.wait(rel["cs"][k - 2])
                SP.dma(lambda e, buf=buf, ti=ti: e.dma_start(out=buf[:, 0, :], in_=self.cos_in[:, ti * TT:(ti + 1) * TT]), s_ldc[k % 2])
                ev = SP.dma(lambda e, buf=buf, ti=ti: e.dma_start(out=buf[:, 1, :], in_=self.sin_in[:, ti * TT:(ti + 1) * TT]), s_ldc[k % 2])
                cs_buf[ti], cs_ev[ti] = buf, ev
                return k

            def emit_qk(ti, cgs, cs_k=None):
                hbuf = hT[ti % 2]
                for cg in cgs:
                    ks, slab, evsl = load_slab(cg)
                    is_q = cg in CG_Q
                    dst = self.qT_d if is_q else self.kT_d
                    cgi = (cg - CG_Q[0]) if is_q else (cg - CG_K[0])
                    for pr in range(2):
                        k = cnt["qk"]; cnt["qk"] += 1
                        banks = (qkb[2 * (k % 2)], qkb[2 * (k % 2) + 1])
                        if k >= 2:
                            PE.wait(rel["qk"][k - 2])
                        PE.wait(evsl)
                        PE.wait(hT_ready[ti])
                        for ch in range(2):
                            c0 = (pr * 2 + ch) * 128
                            for kc in range(KC):
                                fn = lambda e, b=banks[ch], kc=kc, c0=c0, slab=slab, hbuf=hbuf: e.matmul(
                                    b[:, :], lhsT=slab[:, kc, c0:c0 + 128], rhs=hbuf[:, kc, :],
                                    start=(kc == 0), stop=(kc == KC - 1))
                                if kc == KC - 1 and ch == 1:
                                    evq = PE.sig(fn)
                                else:
                                    PE.op(fn)
                        if pr == 1:
                            rel["slab"][ks] = evq
                        hT_pe_done[ti] = evq
                        csb = cs_buf[ti]
                        kr = cnt["rt"]; cnt["rt"] += 1
                        tbuf = rt[kr % 2]
                        if kr >= 2:
                            DVE.wait(rel["rt"][kr - 2])
                        DVE.wait(evq)
                        DVE.wait(cs_ev[ti])
                        for (slot, bk, tb) in ((0, 0, 0), (1, 1, 1), (2, 1, 0), (3, 0, 1)):
                            fn = lambda e, slot=slot, bk=bk, tb=tb, tbuf=tbuf, banks=banks, csb=csb: e.tensor_tensor(
                                out=tbuf[:, slot, :], in0=banks[bk][:, :], in1=csb[:, tb, :], op=ALU.mult)
                            if slot == 3:
                                evr = DVE.sig(fn)
                            else:
                                DVE.op(fn)
                        rel["qk"][k] = evr
                        if cs_k is not None and cg == cgs[-1] and pr == 1:
                            rel["cs"][cs_k] = evr
                        ko = cnt["qko"]; cnt["qko"] += 1
                        obuf = qko[ko % 2]
                        if ko >= 2:
                            POOL.wait(rel["qko"][ko - 2])
                        POOL.wait(evr)
                        POOL.op(lambda e, tbuf=tbuf, obuf=obuf: e.tensor_tensor(
                            out=obuf[:, 0, :], in0=tbuf[:, 0, :], in1=tbuf[:, 1, :], op=ALU.subtract))
                        evo = POOL.sig(lambda e, tbuf=tbuf, obuf=obuf: e.tensor_tensor(
                            out=obuf[:, 1, :], in0=tbuf[:, 2, :], in1=tbuf[:, 3, :], op=ALU.add))
                        rel["rt"][kr] = evo
                        POOL.wait(evo)
                        hA = 2 * (cgi * 2 + pr)
                        for hd in range(2):
                            evst = POOL.dma(lambda e, obuf=obuf, hd=hd, hA=hA, dst=dst, ti=ti: e.dma_start(
                                out=dst[hA + hd, :, ti * TT:(ti + 1) * TT].rearrange("(hf d) t -> d hf t", hf=2),
                                in_=obuf[hd * 64:(hd + 1) * 64, :, :]), s_stq[ko % 2])
                        rel["qko"][ko] = evst

            def emit_v(ti, cgs):
                hbuf = hT[ti % 2]
                t0 = ti * TT
                for ci, cg in enumerate(cgs):
                    ks, slab, evsl = load_slab(cg)
                    for blk in range(NB_T):
                        k = cnt["v"]; cnt["v"] += 1
                        bank = vb[k % 2]
                        if k >= 2:
                            PE.wait(rel["v"][k - 2])
                        PE.wait(evsl)
                        PE.wait(hT_ready[ti])
                        for kc in range(KC):
                            fn = lambda e, bank=bank, kc=kc, blk=blk, slab=slab, hbuf=hbuf: e.matmul(
                                bank[:, :], lhsT=hbuf[:, kc, blk * 128:(blk + 1) * 128], rhs=slab[:, kc, :],
                                start=(kc == 0), stop=(kc == KC - 1))
                            if kc == KC - 1:
                                evv = PE.sig(fn)
                            else:
                                PE.op(fn)
                        if blk == NB_T - 1:
                            rel["slab"][ks] = evv
                        hT_pe_done[ti] = evv
                        ko = cnt["vto"]; cnt["vto"] += 1
                        obuf = vto[ko % 3]
                        if ko >= 3:
                            ACT.wait(rel["vto"][ko - 3])
                        ACT.wait(evv)
                        evo = ACT.sig(lambda e, bank=bank, obuf=obuf: e.activation(out=obuf[:], in_=bank[:, :], func=AF.Copy))
                        rel["v"][k] = evo
                        ACT.wait(evo)
                        r0 = t0 + blk * 128
                        if cg in CG_V:
                            g = cg - CG_V[0]
                            rel["vto"][ko] = ACT.dma(lambda e, obuf=obuf, r0=r0, g=g: e.dma_start(
                                out=self.v_d[PADV + r0:PADV + r0 + 128, g * 512:(g + 1) * 512], in_=obuf[:]), s_stv[ko % 3])
                        else:
                            g = cg - CG_PI[0]
                            rel["vto"][ko] = ACT.dma(lambda e, obuf=obuf, r0=r0, g=g: e.dma_start(
                                out=self.pi_d[r0:r0 + 128, g * 512:(g + 1) * 512], in_=obuf[:]), s_stv[ko % 3])

            emit_norm(0)
            emit_transposes(0)
            for ti in range(NT):
                self.feed2(l, 3)
                ck = load_cs(ti)
                if ti + 1 < NT:
                    emit_norm(ti + 1)
                emit_qk(ti, CG_Q)
                if ti + 1 < NT:
                    emit_transposes(ti + 1)
                emit_qk(ti, CG_K, cs_k=ck)
                emit_v(ti, CG_V + CG_PI)
            for ti in range(max(0, NT - 2), NT):
                ACT.wait(hT_st[ti])
            for ko in range(max(0, cnt["vto"] - 3), cnt["vto"]):
                ACT.wait(rel["vto"][ko])
            for ko in range(max(0, cnt["qko"] - 2), cnt["qko"]):
                POOL.wait(rel["qko"][ko])
            self.emit_block()

    def passA(self, l):
        nc, S = self.nc, self.S
        PE, ACT, DVE, POOL, SP = self.new_queues()
        sem = self.sem
        plan = attn_batches(S)
        SCALE = 1.0 / math.sqrt(128.0)
        with ExitStack() as esA:
            def sb(name, shape, dt):
                return esA.enter_context(nc.sbuf_tensor(f"{name}_L{l}", list(shape), dt))

            def psb(name):
                return esA.enter_context(nc.psum_tensor(f"{name}_L{l}", [128, 512], F32))

            NMV = self.NMV
            masks = sb("pa_masks", [128, NMV, 512], BF16)
            ones = sb("pa_ones", [128, 128], BF16)
            QT = [sb(f"pa_QT{i}", [128, S], BF16) for i in range(2)]
            KT = [sb(f"pa_KT{i}", [128, S + 2 * PADV], BF16) for i in range(2)]
            nblk_max = max(dil * (S // dil // 128 + 1) for (_, dil) in GROUPS)
            VB = [sb(f"pa_V{i}", [128, nblk_max, 128], BF16) for i in range(2)]
            PT = [sb(f"pa_P{i}", [128, 2, 512], BF16) for i in range(3)]
            U = [sb(f"pa_U{i}", [128, S], F32) for i in range(4)]
            LsB = [sb(f"pa_Ls{i}", [128, S], F32) for i in range(2)]
            bo = [sb(f"pa_bo{i}", [128, S], BF16) for i in range(2)]
            stb = [psb(f"pa_st{i}") for i in range(4)]
            otb = [psb(f"pa_ot{i}") for i in range(2)]
            lbb = [psb(f"pa_lb{i}") for i in range(2)]

            s_cst = sem("s_cst")
            s_ldq = self.dma_ring("ld_x", 2)
            s_ldk = self.dma_ring("ld_w", 2)
            s_ldv = self.dma_ring("ld_c", 2)
            s_stb = self.dma_ring("st_h", 2)

            ev_m = SP.dma(lambda e: e.dma_start(out=masks[:], in_=self.mask_in.rearrange("v p c -> p v c")), s_cst)
            POOL.op(lambda e: e.memset(ones[:], 1.0))
            for i in range(2):
                POOL.op(lambda e, i=i: e.memset(KT[i][:, 0:PADV], 0.0))
                ev_pad = POOL.sig(lambda e, i=i: e.memset(KT[i][:, PADV + S:PADV + S + PADV], 0.0))

            heads = [(hh, g) for hh in range(4) for g in range(3)]
            cnt = dict(st=0, p=0, ot=0, bo=0)
            rel = dict(st={}, p={}, ot={}, lb={}, bo={})
            head_rel = {}
            ld_ev = {}

            def load_head(hi):
                hh, g = heads[hi]
                head = 4 * g + hh
                dil = GROUPS[g][1]
                L = S // dil
                nm = L // 128 + 1
                slot = hi % 2
                if hi >= 2:
                    SP.wait(head_rel[hi - 2])
                e1 = SP.dma(lambda e: e.dma_start(out=QT[slot][:, :], in_=self.qT_d[head]), s_ldq[slot])
                e2 = SP.dma(lambda e: e.dma_start(out=KT[slot][:, PADV:PADV + S], in_=self.kT_d[head]), s_ldk[slot])
                e3 = None
                for c in range(dil):
                    r0 = PADV - 64 * dil + c
                    nrows = nm * 128
                    src = self.v_d[r0:r0 + dil * (nrows - 1) + 1:dil, head * 128:(head + 1) * 128]
                    e3 = SP.dma(lambda e, c=c, src=src: e.dma_start(
                        out=VB[slot][:, c * nm:(c + 1) * nm, :], in_=src.rearrange("(m p) d -> p m d", p=128)), s_ldv[slot])
                ld_ev[hi] = (e1, e2, e3)

            flat = []
            for hi, (hh, g) in enumerate(heads):
                for bi, batch in enumerate(plan[g]):
                    flat.append((hi, hh, g, bi, batch))
            sc_state = {}
            u_last = {}
            u_rd, ls_rd = {}, {}
            ls_last = [None]

            def emit_scores(idx):
                hi, hh, g, bi, batch = flat[idx]
                dil = GROUPS[g][1]
                slot = hi % 2
                qt, kt = QT[slot], KT[slot]
                nit = len(batch)
                ks = cnt["st"]; cnt["st"] += 1
                stA, stB = stb[2 * (ks % 2)], stb[2 * (ks % 2) + 1]
                if ks >= 2:
                    PE.wait(rel["st"][ks - 2])
                if bi == 0:
                    for ev in ld_ev[hi]:
                        PE.wait(ev)
                    PE.wait(ev_pad)
                for i, (c, qb) in enumerate(batch):
                    kA = PADV + dil * (128 * qb - 64) + c
                    kB = kA + 128 * dil
                    q0 = dil * 128 * qb + c
                    qs = qt[:, q0:q0 + dil * 127 + 1:dil]
                    for (bank, k0) in ((stA, kA), (stB, kB)):
                        fn = lambda e, bank=bank, k0=k0, i=i, qs=qs, kt=kt, dil=dil: e.matmul(
                            bank[:, i * 128:(i + 1) * 128], lhsT=kt[:, k0:k0 + dil * 127 + 1:dil], rhs=qs,
                            start=True, stop=True)
                        if i == nit - 1 and bank is stB:
                            ev_s = PE.sig(fn)
                        else:
                            PE.op(fn)
                mA, mB = self.midx[g][bi]
                kp = cnt["p"]; cnt["p"] += 1
                pbuf = PT[kp % 3]
                if kp >= 3:
                    ACT.wait(rel["p"][kp - 3])
                ACT.wait(ev_s)
                W = nit * 128
                ACT.op(lambda e, pbuf=pbuf, stA=stA, W=W: e.activation(
                    out=pbuf[:, 0, 0:W], in_=stA[:, 0:W], func=AF.Exp, scale=SCALE))
                ev_e = ACT.sig(lambda e, pbuf=pbuf, stB=stB, W=W: e.activation(
                    out=pbuf[:, 1, 0:W], in_=stB[:, 0:W], func=AF.Exp, scale=SCALE))
                rel["st"][ks] = ev_e
                POOL.wait(ev_e)
                POOL.wait(ev_m)
                ev_pA = POOL.sig(lambda e, pbuf=pbuf, mA=mA, W=W: e.tensor_tensor(
                    out=pbuf[:, 0, 0:W], in0=pbuf[:, 0, 0:W], in1=masks[:, mA, 0:W], op=ALU.mult))
                DVE.wait(ev_e)
                DVE.wait(ev_m)
                ev_p = DVE.sig(lambda e, pbuf=pbuf, mB=mB, W=W: e.tensor_tensor(
                    out=pbuf[:, 1, 0:W], in0=pbuf[:, 1, 0:W], in1=masks[:, mB, 0:W], op=ALU.mult))
                sc_state[idx] = (kp, pbuf, ev_p, ev_pA, W)

            def emit_rest(idx):
                hi, hh, g, bi, batch = flat[idx]
                head = 4 * g + hh
                dil = GROUPS[g][1]
                L = S // dil
                nbc = L // 128
                nm = nbc + 1
                slot = hi % 2
                ui = (3 * hh + g) % 4
                vbuf, ug, Ls = VB[slot], U[ui], LsB[hh % 2]
                mA, mB = self.midx[g][bi]
                nit = len(batch)
                kp, pbuf, ev_p, ev_pA, W = sc_state.pop(idx)
                if bi == 0 and hi >= 1 and hi + 1 < len(heads):
                    load_head(hi + 1)
                ko = cnt["ot"]; cnt["ot"] += 1
                ob, lb = otb[ko % 2], lbb[ko % 2]
                if ko >= 2:
                    for ev in rel["ot"][ko - 2]:
                        PE.wait(ev)
                PE.wait(ev_p)
                PE.wait(ev_pA)
                for i, (c, qb) in enumerate(batch):
                    blkA = c * nm + qb
                    for (half, blk) in ((0, blkA), (1, blkA + 1)):
                        PE.op(lambda e, ob=ob, i=i, half=half, blk=blk, vbuf=vbuf, pbuf=pbuf: e.matmul(
                            ob[:, i * 128:(i + 1) * 128], lhsT=vbuf[:, blk, :], rhs=pbuf[:, half, i * 128:(i + 1) * 128],
                            start=(half == 0), stop=(half == 1)))
                for i, (c, qb) in enumerate(batch):
                    for half in (0, 1):
                        fn = lambda e, lb=lb, i=i, half=half, pbuf=pbuf: e.matmul(
                            lb[:, i * 128:(i + 1) * 128], lhsT=ones[:, :], rhs=pbuf[:, half, i * 128:(i + 1) * 128],
                            start=(half == 0), stop=(half == 1))
                        if i == nit - 1 and half == 1:
                            ev_o = PE.sig(fn)
                        else:
                            PE.op(fn)
                rel["p"][kp] = ev_o
                c0, qb0 = batch[0]
                ncls = len(set(c for c, _ in batch))
                nq = nit // ncls
                ugv = ug[:, :].rearrange("p (j d) -> p d j", d=dil)[:, c0:c0 + ncls, qb0 * 128:(qb0 + nq) * 128]
                lsv = Ls[:, :].rearrange("p (j d) -> p d j", d=dil)[:, c0:c0 + ncls, qb0 * 128:(qb0 + nq) * 128]
                obv = ob[:, 0:W].rearrange("p (c j) -> p c j", c=ncls)
                lbv = lb[:, 0:W].rearrange("p (c j) -> p c j", c=ncls)
                DVE.wait(ev_o)
                ACT.wait(ev_o)
                if bi == 0:
                    ACT.wait(u_rd.get(ui))
                    if g == 0:
                        for ev in ls_rd.get(hh % 2, ()):
                            DVE.wait(ev)
                ev_u = ACT.sig(lambda e, ugv=ugv, obv=obv: e.activation(out=ugv, in_=obv, func=AF.Copy))
                u_last[g] = ev_u
                if g == 0:
                    ev_d = DVE.sig(lambda e, lsv=lsv, lbv=lbv: e.tensor_copy(out=lsv, in_=lbv))
                else:
                    if bi == 0:
                        DVE.wait(ls_last[0])
                    ev_d = DVE.sig(lambda e, lsv=lsv, lbv=lbv: e.tensor_tensor(out=lsv, in0=lbv, in1=lsv, op=ALU.add))
                ls_last[0] = ev_d
                rel["ot"][ko] = (ev_d, ev_u)
                if bi == len(plan[g]) - 1:
                    head_rel[hi] = ev_o
                    if g == 2:
                        ACT.wait(ev_d)
                        ev_ln = ACT.sig(lambda e, Ls=Ls: e.activation(out=Ls[:, :], in_=Ls[:, :], func=AF.Ln))
                        ACT.wait(ev_ln)
                        ev_r = ACT.sig(lambda e, Ls=Ls: e.activation(out=Ls[:, :], in_=Ls[:, :], func=AF.Exp, scale=-1.0))
                        DVE.wait(ev_r)
                        POOL.wait(ev_r)
                        frees = []
                        for gg in range(3):
                            kb = cnt["bo"]; cnt["bo"] += 1
                            bb = bo[kb % 2]
                            eng = POOL if gg == 1 else DVE
                            eng.wait(u_last[gg])
                            if kb >= 2:
                                eng.wait(rel["bo"][kb - 2])
                            ugg = (3 * hh + gg) % 4
                            ev_b = eng.sig(lambda e, ugg=ugg, bb=bb, Ls=Ls: e.tensor_tensor(
                                out=bb[:, :], in0=U[ugg][:, :], in1=Ls[:, :], op=ALU.mult))
                            u_rd[ugg] = ev_b
                            frees.append(ev_b)
                            ACT.wait(ev_b)
                            hd = 4 * gg + hh
                            rel["bo"][kb] = ACT.dma(lambda e, bb=bb, hd=hd: e.dma_start(out=self.bT_d[hd], in_=bb[:, :]), s_stb[kb % 2])
                        ls_rd[hh % 2] = frees

            load_head(0)
            load_head(1)
            emit_scores(0)
            for idx in range(len(flat)):
                if idx % 12 == 0:
                    self.feed2(l, 1)
                if idx + 1 < len(flat):
                    emit_scores(idx + 1)
                emit_rest(idx)
            for kb in range(max(0, cnt["bo"] - 2), cnt["bo"]):
                ACT.wait(rel["bo"][kb])
            self.emit_block()

    def pass2(self, l):
        nc, S, NT, NB = self.nc, self.S, self.NT, self.NB
        PE, ACT, DVE, POOL, SP = self.new_queues()
        sem = self.sem
        x_src = self.x_in if l == 0 else self.x_d
        with ExitStack() as es2:
            def sb(name, shape, dt):
                return es2.enter_context(nc.sbuf_tensor(f"{name}_L{l}", list(shape), dt))

            hTt = sb("p2_hT", [128, KC, TT], BF16)
            BTt = sb("p2_BT", [128, NH, TT], BF16)
            PIw = sb("p2_PI", [128, NB_T + 2, POOLW], BF16)
            slabs = [sb(f"p2_slab{i}", [128, KC, 512], BF16) for i in range(3)]
            ptmp = sb("p2_ptmp", [128, 16, TT], BF16)
            AT = sb("p2_AT", [128, 8, TT], BF16)
            SG = [sb(f"p2_SG{i}", [128, TT], BF16) for i in range(4)]
            M1 = sb("p2_M1", [128, KC, TT], BF16)
            T2 = [sb(f"p2_T2{i}", [128, TT], F32) for i in range(2)]
            MT = sb("p2_MT", [128, KC, TT], BF16)
            GT = sb("p2_GT", [128, D], F32)
            XP = [sb(f"p2_XP{i}", [128, 512], F32) for i in range(4)]
            T3 = [sb(f"p2_T3{i}", [128, 512], F32) for i in range(2)]
            Wg = sb("p2_Wg", [128, 4, 2, 256], BF16)
            Bm = sb("p2_Bm", [128, 48, 128], BF16)
            psc = sb("p2_psc", [128, 8], F32)
            banks = [es2.enter_context(nc.psum_tensor(f"p2_b{i}_L{l}", [128, 512], F32)) for i in range(8)]

            s_cst = sem("s_cst")
            s_ldw = self.dma_ring("ld_w", 3)
            s_ldh, s_ldp, s_ldb = sem("ld_h2"), sem("ld_p2"), sem("ld_b2")
            s_ldx = self.dma_ring("ld_x", 4)
            s_stx = self.dma_ring("st_x", 4)

            self.feed_cast(l, 1000)
            self.wait_casts(SP, l)
            SP.dma(lambda e: e.dma_start(out=Wg[:], in_=self.wg_bf[l]), s_cst)
            SP.dma(lambda e: e.dma_start(out=Bm[:], in_=self.pmat_in.rearrange("s w r p t -> p (s w r) t")), s_cst)
            SP.dma(lambda e: e.dma_start(out=psc[:], in_=self.psc_in[l].rearrange("(oc p) -> p oc", p=128),
                                         allow_slow_non_contiguous=True), s_cst)
            ev_cst = SP.dma(lambda e: e.dma_start(out=GT[:], in_=self.mod_d[l, 2].partition_broadcast(128)), s_cst)


            cnt = dict(slab=0, bank=0, sg=0, t2=0, xp=0, t3=0)
            rel = dict(slab={}, bank={}, sg={}, t2={}, xp={}, t3={})
            ld = {}
            tile_ev = {}

            def load_tile(ti):
                Q = ACT if ti > 0 else SP
                prev = tile_ev.get(ti - 1, {})
                b0 = ti * NB_T
                Q.wait(prev.get("hT_done"))
                e_h = Q.dma(lambda e: e.dma_start(
                    out=hTt[:, :, :], in_=self.hT_d[:, :, ti * TT:(ti + 1) * TT].rearrange("kc p t -> p kc t")), s_ldh)
                Q.wait(prev.get("PI_done"))
                w0 = 1 if b0 == 0 else 0
                w1 = NB_T + 1 if b0 + NB_T == NB else NB_T + 2
                r0 = (b0 - 1 + w0) * 128
                e_p = Q.dma(lambda e: e.dma_start(
                    out=PIw[:, w0:w1, :], in_=self.pi_d[r0:r0 + (w1 - w0) * 128, :].rearrange("(w p) c -> p w c", p=128)), s_ldp)
                Q.wait(prev.get("BT_done"))
                e_b = Q.dma(lambda e: e.dma_start(
                    out=BTt[:, :, :], in_=self.bT_d[:, :, ti * TT:(ti + 1) * TT].rearrange("h p t -> p h t")), s_ldb)
                ld[ti] = (e_h, e_p, e_b)

            def load_slab(src_ap, kcn):
                k = cnt["slab"]; cnt["slab"] += 1
                buf = slabs[k % 3]
                if k >= 3:
                    SP.wait(rel["slab"][k - 3])
                ev = SP.dma(lambda e, buf=buf, src_ap=src_ap, kcn=kcn: e.dma_start(out=buf[:, 0:kcn, :], in_=src_ap), s_ldw[k % 3])
                return k, buf, ev

            def get_bank():
                k = cnt["bank"]; cnt["bank"] += 1
                if k >= 8:
                    PE.wait(rel["bank"][k - 8])
                return k, banks[k % 8]

            def mm_group(bank, pairs, waits=()):
                for w in waits:
                    PE.wait(w)
                n = len(pairs)
                ev = None
                for i, (lhsT, rhs) in enumerate(pairs):
                    fn = lambda e, lhsT=lhsT, rhs=rhs, i=i: e.matmul(bank[:, :], lhsT=lhsT, rhs=rhs, start=(i == 0), stop=(i == n - 1))
                    if i == n - 1:
                        ev = PE.sig(fn)
                    else:
                        PE.op(fn)
                return ev

            def get_sg():
                k = cnt["sg"]; cnt["sg"] += 1
                if k >= 4:
                    ACT.wait(rel["sg"][k - 4])
                return k, SG[k % 4]

            SGP = lambda oc: ptmp[:, oc, :]
            PTs = lambda pc: ptmp[:, 8 + pc, :]
            BG = lambda j: ptmp[:, j, :]

            load_tile(0)
            for ti in range(NT):
                self.feed_cast(l + 1, 2)
                b0 = ti * NB_T
                e_h, e_p, e_b = ld[ti]
                tev = tile_ev.setdefault(ti, {})
                prev = tile_ev.get(ti - 1, {})
                for cg in CG_PG:
                    ks, slab, evsl = load_slab(self.wi_bf[l, cg], KC)
                    for ch in range(4):
                        oc = (cg - CG_PG[0]) * 4 + ch
                        kb, bank = get_bank()
                        ev = mm_group(bank, [(slab[:, kc, ch * 128:(ch + 1) * 128], hTt[:, kc, :]) for kc in range(KC)],
                                      waits=(evsl, e_h))
                        ACT.wait(ev)
                        ACT.wait(prev.get("ptmp_done"))
                        rel["bank"][kb] = ACT.sig(lambda e, bank=bank, oc=oc: e.activation(out=SGP(oc), in_=bank[:, :], func=AF.Silu))
                    rel["slab"][ks] = ev
                ev_sgp = rel["bank"][kb]
                ev_pt = None
                for pc in range(8):
                    g = pc // 2
                    kb, bank = get_bank()
                    PE.wait(e_p)
                    PE.wait(ev_cst)
                    for blk in range(NB_T):
                        b = b0 + blk
                        st = 0 if b == 0 else (3 if b == NB - 1 else (2 if b == NB // 2 - 1 else 1))
                        rels = [r for r in (0, 1, 2) if not ((r == 0 and b == 0) or (r == 2 and b == NB - 1))]
                        for ri, r in enumerate(rels):
                            fn = lambda e, bank=bank, blk=blk, r=r, pc=pc, st=st, g=g, ri=ri, nr=len(rels): e.matmul(
                                bank[:, blk * 128:(blk + 1) * 128], lhsT=PIw[:, blk + r, pc * 128:(pc + 1) * 128],
                                rhs=Bm[:, (st * 4 + g) * 3 + r, :], start=(ri == 0), stop=(ri == nr - 1))
                            if blk == NB_T - 1 and ri == len(rels) - 1:
                                ev = PE.sig(fn)
                            else:
                                PE.op(fn)
                    DVE.wait(ev)
                    DVE.wait(prev.get("ptmp_done"))
                    ev_pt = DVE.sig(lambda e, bank=bank, pc=pc: e.tensor_copy(out=PTs(pc), in_=bank[:, :]))
                    rel["bank"][kb] = ev_pt
                tev["PI_done"] = ev
                for oc in range(8):
                    g = oc // 2
                    kb, bank = get_bank()
                    ev = mm_group(bank, [(Wg[:, g, k2, (oc % 2) * 128:(oc % 2 + 1) * 128], PTs(2 * g + k2)) for k2 in range(2)],
                                  waits=(ev_pt, ev_cst))
                    DVE.wait(ev)
                    DVE.wait(ev_sgp)
                    DVE.wait(prev.get("AT_done"))
                    ev_at = DVE.sig(lambda e, bank=bank, oc=oc: e.scalar_tensor_tensor(
                        out=AT[:, oc, :], in0=bank[:, :], scalar=psc[:, oc:oc + 1], in1=SGP(oc), op0=ALU.mult, op1=ALU.mult))
                    rel["bank"][kb] = ev_at
                ev_p3_pe = ev
                for og in range(4):
                    ks2, s2, ev2 = load_slab(self.wi_bf[l, CG_GP[og]], KC)
                    ks1, s1, ev1 = load_slab(self.wpp_bf[l, og], 8)
                    for ch in range(4):
                        f = og * 4 + ch
                        kbG, bankG = get_bank()
                        evG = mm_group(bankG, [(s2[:, kc, ch * 128:(ch + 1) * 128], hTt[:, kc, :]) for kc in range(KC)],
                                       waits=(ev2, e_h))
                        kbA, bankA = get_bank()
                        evA = mm_group(bankA, [(s1[:, kc, ch * 128:(ch + 1) * 128], AT[:, kc, :]) for kc in range(8)],
                                       waits=(ev1, ev_at))
                        ksg, sg = get_sg()
                        ACT.wait(evG)
                        ev_sg = ACT.sig(lambda e, bankG=bankG, sg=sg: e.activation(out=sg[:, :], in_=bankG[:, :], func=AF.Sigmoid))
                        rel["bank"][kbG] = ev_sg
                        DVE.wait(evA)
                        DVE.wait(ev_sg)
                        DVE.wait(prev.get("M1_done"))
                        ev_m1 = DVE.sig(lambda e, bankA=bankA, sg=sg, f=f: e.tensor_tensor(
                            out=M1[:, f, :], in0=bankA[:, :], in1=sg[:, :], op=ALU.mult))
                        rel["bank"][kbA] = ev_m1
                        rel["sg"][ksg] = ev_m1
                    rel["slab"][ks1] = evA
                    rel["slab"][ks2] = evG
                tev["AT_done"] = evA
                ev_bg = None
                for cg in CG_AG:
                    ks, slab, evsl = load_slab(self.wi_bf[l, cg], KC)
                    for ch in range(4):
                        j = (cg - CG_AG[0]) * 4 + ch
                        kb, bank = get_bank()
                        ev = mm_group(bank, [(slab[:, kc, ch * 128:(ch + 1) * 128], hTt[:, kc, :]) for kc in range(KC)],
                                      waits=(evsl, e_h))
                        ksg, sg = get_sg()
                        ACT.wait(ev)
                        ev_sg = ACT.sig(lambda e, bank=bank, sg=sg: e.activation(out=sg[:, :], in_=bank[:, :], func=AF.Silu))
                        rel["bank"][kb] = ev_sg
                        POOL.wait(ev_sg)
                        POOL.wait(e_b)
                        POOL.wait(ev_at)
                        POOL.wait(ev_p3_pe)
                        POOL.wait(prev.get("BG_done"))
                        ev_bg = POOL.sig(lambda e, sg=sg, j=j: e.tensor_tensor(out=BG(j), in0=sg[:, :], in1=BTt[:, j, :], op=ALU.mult))
                        rel["sg"][ksg] = ev_bg
                    rel["slab"][ks] = ev
                tev["BT_done"] = ev_bg
                for og in range(4):
                    ks2, s2, ev2 = load_slab(self.wi_bf[l, CG_GA[og]], KC)
                    ks1, s1, ev1 = load_slab(self.wpa_bf[l, og], NH)
                    for ch in range(4):
                        f = og * 4 + ch
                        kbG, bankG = get_bank()
                        evG = mm_group(bankG, [(s2[:, kc, ch * 128:(ch + 1) * 128], hTt[:, kc, :]) for kc in range(KC)],
                                       waits=(ev2, e_h))
                        kbB, bankB = get_bank()
                        evB = mm_group(bankB, [(s1[:, kc, ch * 128:(ch + 1) * 128], BG(kc)) for kc in range(NH)],
                                       waits=(ev1, ev_bg))
                        ksg, sg = get_sg()
                        ACT.wait(evG)
                        ev_sg = ACT.sig(lambda e, bankG=bankG, sg=sg: e.activation(out=sg[:, :], in_=bankG[:, :], func=AF.Sigmoid))
                        rel["bank"][kbG] = ev_sg
                        kt = cnt["t2"]; cnt["t2"] += 1
                        t2 = T2[kt % 2]
                        if kt >= 2:
                            DVE.wait(rel["t2"][kt - 2])
                        DVE.wait(evB)
                        DVE.wait(ev_sg)
                        ev_t2 = DVE.sig(lambda e, bankB=bankB, sg=sg, t2=t2: e.tensor_tensor(
                            out=t2[:, :], in0=bankB[:, :], in1=sg[:, :], op=ALU.mult))
                        rel["bank"][kbB] = ev_t2
                        rel["sg"][ksg] = ev_t2
                        POOL.wait(ev_t2)
                        POOL.wait(ev_m1)
                        POOL.wait(prev.get("MT_done"))
                        ev_mt = POOL.sig(lambda e, t2=t2, f=f: e.tensor_tensor(out=MT[:, f, :], in0=t2[:, :], in1=M1[:, f, :], op=ALU.add))
                        rel["t2"][kt] = ev_mt
                    rel["slab"][ks1] = evB
                    rel["slab"][ks2] = evG
                tev["hT_done"] = evB
                tev["BG_done"] = evB
                tev["ptmp_done"] = evB
                tev["M1_done"] = ev_mt
                if ti + 1 < NT:
                    load_tile(ti + 1)
                for og in range(4):
                    ks, slab, evsl = load_slab(self.wo_bf[l, og], KC)
                    for blk in range(NB_T):
                        r0 = (b0 + blk) * 128
                        kx = cnt["xp"]; cnt["xp"] += 1
                        xp = XP[kx % 4]
                        if kx >= 4:
                            SP.wait(rel["xp"][kx - 4])
                        ev_x = SP.dma(lambda e, xp=xp, r0=r0, og=og: e.dma_start(
                            out=xp[:, :], in_=x_src[r0:r0 + 128, og * 512:(og + 1) * 512]), s_ldx[kx % 4])
                        kb, bank = get_bank()
                        ev = mm_group(bank, [(MT[:, kc, blk * 128:(blk + 1) * 128], slab[:, kc, :]) for kc in range(KC)],
                                      waits=(evsl, ev_mt))
                        k3 = cnt["t3"]; cnt["t3"] += 1
                        t3 = T3[k3 % 2]
                        if k3 >= 2:
                            DVE.wait(rel["t3"][k3 - 2])
                        DVE.wait(ev)
                        DVE.wait(ev_cst)
                        ev_t3 = DVE.sig(lambda e, bank=bank, t3=t3, og=og: e.tensor_tensor(
                            out=t3[:, :], in0=bank[:, :], in1=GT[:, og * 512:(og + 1) * 512], op=ALU.mult))
                        rel["bank"][kb] = ev_t3
                        POOL.wait(ev_t3)
                        POOL.wait(ev_x)
                        ev_xn = POOL.sig(lambda e, t3=t3, xp=xp: e.tensor_tensor(out=xp[:, :], in0=t3[:, :], in1=xp[:, :], op=ALU.add))
                        rel["t3"][k3] = ev_xn
                        POOL.wait(ev_xn)
                        rel["xp"][kx] = POOL.dma(lambda e, xp=xp, r0=r0, og=og: e.dma_start(
                            out=self.x_d[r0:r0 + 128, og * 512:(og + 1) * 512], in_=xp[:, :]), s_stx[kx % 4])
                    rel["slab"][ks] = ev
                tev["MT_done"] = ev
            for kx in range(max(0, cnt["xp"] - 4), cnt["xp"]):
                POOL.wait(rel["xp"][kx])
            self.emit_block()

    def passF(self):
        nc, S, NB = self.nc, self.S, self.NB
        PE, ACT, DVE, POOL, SP = self.new_queues()
        sem = self.sem
        with ExitStack() as esf:
            def sb(name, shape, dt):
                return esf.enter_context(nc.sbuf_tensor(name, list(shape), dt))
            FG = sb("pf_FG", [128, D], F32)
            eps_t = sb("pf_eps", [128, 1], F32)
            xb = [sb(f"pf_x{i}", [128, D], F32) for i in range(4)]
            yb = [sb(f"pf_y{i}", [128, D], F32) for i in range(3)]
            junk = sb("pf_junk", [128, D], BF16)
            stat = sb("pf_stat", [128, 32], F32)
            s_cst = sem("s_cst")
            s_ldx = self.dma_ring("ld_x", 4)
            s_sty = self.dma_ring("st_y", 3)
            ev_cst = SP.dma(lambda e: e.dma_start(out=FG[:], in_=self.fg_in.partition_broadcast(128)), s_cst)
            ev_eps = POOL.sig(lambda e: e.memset(eps_t[:], EPS))
            ACT.wait(ev_eps)
            relx, rely = {}, {}
            for b in range(NB):
                xbuf, ybuf = xb[b % 4], yb[b % 3]
                if b >= 4:
                    SP.wait(relx[b - 4])
                evx = SP.dma(lambda e, xbuf=xbuf, b=b: e.dma_start(out=xbuf[:], in_=self.x_d[b * 128:(b + 1) * 128, :]), s_ldx[b % 4])
                col = (b % 8) * 3
                ACT.wait(evx)
                eva = ACT.sig(lambda e, xbuf=xbuf, col=col: e.activation(
                    out=junk[:], in_=xbuf[:], func=AF.Square, accum_out=stat[:, col:col + 1]))
                ACT.wait(eva)
                evs = ACT.sig(lambda e, col=col: e.activation(
                    out=stat[:, col + 1:col + 2], in_=stat[:, col:col + 1], func=AF.Sqrt, bias=eps_t[:, 0:1], scale=1.0 / D))
                DVE.wait(evs)
                evd = DVE.sig(lambda e, col=col: e.reciprocal(out=stat[:, col + 2:col + 3], in_=stat[:, col + 1:col + 2]))
                DVE.wait(evd)
                DVE.wait(ev_cst)
                if b >= 3:
                    DVE.wait(rely[b - 3])
                evy = DVE.sig(lambda e, xbuf=xbuf, ybuf=ybuf, col=col: e.scalar_tensor_tensor(
                    out=ybuf[:], in0=xbuf[:], scalar=stat[:, col + 2:col + 3], in1=FG[:], op0=ALU.mult, op1=ALU.mult))
                relx[b] = evy
                ACT.wait(evy)
                rely[b] = ACT.dma(lambda e, ybuf=ybuf, b=b: e.dma_start(out=self.y_out[b * 128:(b + 1) * 128, :], in_=ybuf[:]), s_sty[b % 3])
            for b in range(max(0, NB - 3), NB):
                ACT.wait(rely[b])
            self.emit_block()

    def finish(self):
        self.es.close()
        return self.nc


def core_inputs(w, x, c, S, s_real):
    xs = np.zeros((S, D), np.float32)
    xs[:x.shape[0]] = x
    perm = qk_perm()
    cols = np.arange(INW)
    cols[2048:2048 + ATTW] = 2048 + perm
    cols[2048 + ATTW:2048 + 2 * ATTW] = 2048 + ATTW + perm
    w_in_p = w["w_in_p"] if "w_in_p" in w else np.ascontiguousarray(w["w_in"][:, :, cols])
    cos, sin = rope_tables(S)
    return {
        "x": xs, "c": np.ascontiguousarray(c, dtype=np.float32),
        "norm_gain": w["norm_gain"], "w_ada": w["w_ada"], "b_ada": w["b_ada"], "w_in": w_in_p,
        "w_pool_grp": w["w_pool_grp"], "pool_scale": w["pool_scale"], "w_proj_pool": w["w_proj_pool"],
        "w_proj_attn": w["w_proj_attn"], "w_out": w["w_out"], "final_gain": w["final_gain"],
        "rope_cos": cos, "rope_sin": sin, "pool_mats": pool_mats(S, s_real), "attn_masks": mask_table(S, s_real),
        "ident": np.eye(128, dtype=np.float32),
        "tile_valid": np.ascontiguousarray(np.broadcast_to(
            ((np.arange(S // TT) * TT) < s_real).astype(np.float32)[None, :], (128, S // TT))),
    }


WEIGHT_NAMES = ("norm_gain", "w_ada", "b_ada", "w_in", "w_pool_grp", "pool_scale", "w_proj_pool", "w_proj_attn",
                "w_out", "final_gain")


def build_program(S=4096, depth=4):
    P = Prog(S=S, depth=depth)
    P.phase0()
    for l in range(depth):
        P.pass1(l)
        P.passA(l)
        P.pass2(l)
    P.passF()
    return P.finish()


def kernel(x_prompt, x_sample, c_prompt, c_sample, norm_gain, w_ada, b_ada, w_in, w_pool_grp, pool_scale,
           w_proj_pool, w_proj_attn, w_out, final_gain):
    S = 4096
    depth = int(np.shape(w_in)[0])
    loc = dict(norm_gain=norm_gain, w_ada=w_ada, b_ada=b_ada, w_in=w_in, w_pool_grp=w_pool_grp, pool_scale=pool_scale,
               w_proj_pool=w_proj_pool, w_proj_attn=w_proj_attn, w_out=w_out, final_gain=final_gain)
    w = {k: np.ascontiguousarray(np.asarray(v), dtype=np.float32) for k, v in loc.items()}
    perm = qk_perm()
    cols = np.arange(INW)
    cols[2048:2048 + ATTW] = 2048 + perm
    cols[2048 + ATTW:2048 + 2 * ATTW] = 2048 + ATTW + perm
    w["w_in_p"] = np.ascontiguousarray(w["w_in"][:, :, cols])
    x_prompt = np.asarray(x_prompt, dtype=np.float32)
    x_sample = np.asarray(x_sample, dtype=np.float32)
    c_prompt = np.asarray(c_prompt, dtype=np.float32)
    c_sample = np.asarray(c_sample, dtype=np.float32)
    nsm, npr = x_sample.shape[0], x_prompt.shape[0]
    order = [("s", 0), ("s", 1), ("p", 0), ("p", 1), ("s", 2), ("s", 3), ("p", 2), ("p", 3)]
    maps = []
    for kind, i in order:
        if kind == "s":
            maps.append(core_inputs(w, x_sample[i], c_sample[i], S, x_sample.shape[1]))
        else:
            maps.append(core_inputs(w, x_prompt[i], c_prompt[i], S, x_prompt.shape[1]))
    nc = build_program(S=S, depth=depth)
    res = run_bass_kernel_spmd(nc, maps, core_ids=list(range(len(maps))))
    ys = [np.asarray(r["y"], dtype=np.float32) for r in res.results]
    y_sample = np.zeros(x_sample.shape, np.float32)
    y_prompt = np.zeros(x_prompt.shape, np.float32)
    for ci, (kind, i) in enumerate(order):
        if kind == "s":
            y_sample[i] = ys[ci][:x_sample.shape[1]]
        else:
            y_prompt[i] = ys[ci][:x_prompt.shape[1]]
    return (y_prompt, y_sample)
```
